# Optimizing a Trainium2 kernel written in Bass

```python
import jax, jax.numpy as jnp
from jax import lax
import numpy as np

D_MODEL = 1024
BATCH = 4
SEQ = 4096
DEPTH = 1

ATTN_HEAD_DIM = 64
ATTN_HEADS = D_MODEL // ATTN_HEAD_DIM
ATTN_KV_HEADS = ATTN_HEADS // 4
ATTN_GROUP = ATTN_HEADS // ATTN_KV_HEADS
WINDOW = 128

MLSTM_HEADS = 8
MLSTM_V_DIM = D_MODEL // MLSTM_HEADS
MLSTM_QK_DIM = MLSTM_V_DIM // 2
MLSTM_CHUNK = 64
CONV_WIDTH = 4
GATE_SOFTCAP = 15.0

D_FF = ((8 * D_MODEL // 3 + 127) // 128) * 128
FFN_RESIDUAL_WEIGHT = 0.5
NORM_EPS = 1e-6

SZ_AQ = ATTN_HEADS * ATTN_HEAD_DIM
SZ_AK = ATTN_KV_HEADS * ATTN_HEAD_DIM
SZ_AV = ATTN_KV_HEADS * ATTN_HEAD_DIM
SZ_MQ = MLSTM_HEADS * MLSTM_QK_DIM
SZ_MK = MLSTM_HEADS * MLSTM_QK_DIM
SZ_MV = MLSTM_HEADS * MLSTM_V_DIM
SZ_MO = MLSTM_HEADS * MLSTM_V_DIM
SZ_MI = MLSTM_HEADS
SZ_MF = MLSTM_HEADS
SZ_GATES = 2 * D_MODEL
SPLIT_POINTS = (
    SZ_AQ,
    SZ_AQ + SZ_AK,
    SZ_AQ + SZ_AK + SZ_AV,
    SZ_AQ + SZ_AK + SZ_AV + SZ_MQ,
    SZ_AQ + SZ_AK + SZ_AV + SZ_MQ + SZ_MK,
    SZ_AQ + SZ_AK + SZ_AV + SZ_MQ + SZ_MK + SZ_MV,
    SZ_AQ + SZ_AK + SZ_AV + SZ_MQ + SZ_MK + SZ_MV + SZ_MO,
    SZ_AQ + SZ_AK + SZ_AV + SZ_MQ + SZ_MK + SZ_MV + SZ_MO + SZ_MI,
    SZ_AQ + SZ_AK + SZ_AV + SZ_MQ + SZ_MK + SZ_MV + SZ_MO + SZ_MI + SZ_MF,
)
F_OFFSET = SPLIT_POINTS[7]
IN_WIDTH = SPLIT_POINTS[8] + SZ_GATES

kernel_name = "hybrid_swa_mlstm_macaron_block"


def rms_norm(x, g):
    xf = x.astype(jnp.float32)
    y = xf * lax.rsqrt(jnp.mean(xf * xf, axis=-1, keepdims=True) + NORM_EPS)
    return (y * g.astype(jnp.float32)).astype(x.dtype)


def swiglu(x, w_gate, w_up, w_down):
    return (jax.nn.silu(x @ w_gate) * (x @ w_up)) @ w_down


def causal_depthwise_conv(x, w):
    s = x.shape[1]
    xp = jnp.pad(x, ((0, 0), (CONV_WIDTH - 1, 0), (0, 0)))
    out = xp[:, 0:s] * w[0]
    for j in range(1, CONV_WIDTH):
        out = out + xp[:, j:j + s] * w[j]
    return out


def sliding_window_attention(q, k, v, sinks):
    b, s = q.shape[0], q.shape[1]
    nb = s // WINDOW
    qb = q.reshape(b, nb, WINDOW, ATTN_KV_HEADS, ATTN_GROUP, ATTN_HEAD_DIM)

    def band(t):
        tp = jnp.pad(t, ((0, 0), (WINDOW, 0), (0, 0), (0, 0)))
        tp = tp.reshape(b, nb + 1, WINDOW, ATTN_KV_HEADS, ATTN_HEAD_DIM)
        return jnp.concatenate([tp[:, :-1], tp[:, 1:]], axis=2)

    kb, vb = band(k), band(v)
    scale = ATTN_HEAD_DIM ** -0.5
    scores = jnp.einsum('bnqhgd,bnkhd->bhgnqk', qb, kb).astype(jnp.float32) * scale
    qi = jnp.arange(WINDOW)[:, None]
    kj = jnp.arange(2 * WINDOW)[None, :]
    in_window = (kj > qi) & (kj <= qi + WINDOW)
    is_pad = (jnp.arange(nb)[:, None, None] == 0) & (kj < WINDOW)[None]
    valid = in_window[None] & ~is_pad
    scores = jnp.where(valid, scores, -jnp.inf)
    sink = jnp.broadcast_to(
        sinks.astype(jnp.float32).reshape(ATTN_KV_HEADS, ATTN_GROUP)[None, :, :, None, None, None],
        scores.shape[:-1] + (1,))
    probs = jax.nn.softmax(jnp.concatenate([scores, sink], axis=-1), axis=-1)[..., :-1]
    out = jnp.einsum('bhgnqk,bnkhd->bnqhgd', probs.astype(v.dtype), vb)
    return out.reshape(b, s, ATTN_HEADS * ATTN_HEAD_DIM)


def mlstm_chunkwise(q, k, v, i_pre, f_pre):
    b, s = q.shape[0], q.shape[1]
    nc = s // MLSTM_CHUNK
    f32 = jnp.float32

    def to_chunks(t):
        return t.astype(f32).reshape(b, nc, MLSTM_CHUNK, MLSTM_HEADS, t.shape[-1]).transpose(1, 0, 3, 2, 4)

    def gate_chunks(t):
        return t.astype(f32).reshape(b, nc, MLSTM_CHUNK, MLSTM_HEADS).transpose(1, 0, 3, 2)

    qc, kc, vc = to_chunks(q), to_chunks(k), to_chunks(v)
    ic = gate_chunks(i_pre)
    lfc = jax.nn.log_sigmoid(gate_chunks(f_pre))
    causal = jnp.tril(jnp.ones((MLSTM_CHUNK, MLSTM_CHUNK), dtype=bool))

    def step(carry, inp):
        c_prev, n_prev, m_prev = carry
        q_, k_, v_, i_, lf_ = inp
        cum = jnp.cumsum(lf_, axis=-1)
        logw = cum[..., :, None] - cum[..., None, :] + i_[..., None, :]
        logw = jnp.where(causal, logw, -jnp.inf)
        log_inter = cum + m_prev[..., None]
        m_t = jnp.maximum(log_inter, jnp.max(logw, axis=-1))
        w_intra = jnp.exp(logw - m_t[..., None])
        w_inter = jnp.exp(log_inter - m_t)
        sc = jnp.einsum('bhtd,bhsd->bhts', q_, k_) * w_intra
        num = jnp.einsum('bhts,bhsv->bhtv', sc, v_) + w_inter[..., None] * jnp.einsum('bhtd,bhdv->bhtv', q_, c_prev)
        den = jnp.sum(sc, axis=-1) + w_inter * jnp.einsum('bhtd,bhd->bht', q_, n_prev)
        h = num / jnp.maximum(jnp.abs(den), jnp.exp(-m_t))[..., None]
        total = cum[..., -1]
        log_k = total[..., None] - cum + i_
        m_new = jnp.maximum(total + m_prev, jnp.max(log_k, axis=-1))
        wk = jnp.exp(log_k - m_new[..., None])
        decay = jnp.exp(total + m_prev - m_new)
        c_new = decay[..., None, None] * c_prev + jnp.einsum('bhs,bhsd,bhsv->bhdv', wk, k_, v_)
        n_new = decay[..., None] * n_prev + jnp.einsum('bhs,bhsd->bhd', wk, k_)
        return (c_new, n_new, m_new), h

    init = (jnp.zeros((b, MLSTM_HEADS, MLSTM_QK_DIM, MLSTM_V_DIM), f32),
            jnp.zeros((b, MLSTM_HEADS, MLSTM_QK_DIM), f32),
            jnp.zeros((b, MLSTM_HEADS), f32))
    _, hs = lax.scan(step, init, (qc, kc, vc, ic, lfc))
    return hs.transpose(1, 0, 3, 2, 4).reshape(b, s, MLSTM_HEADS, MLSTM_V_DIM)


def softcap(t):
    return GATE_SOFTCAP * jnp.tanh(t / GATE_SOFTCAP)


def setup_inputs(seed: int = 0) -> dict:
    key = jax.random.key(seed)
    ks = jax.random.split(key, 24)
    f32 = jnp.float32

    def dense(k, fan_in, fan_out):
        return jax.random.normal(k, (DEPTH, fan_in, fan_out), f32) * fan_in ** -0.5

    def gain(k, shape):
        return 1.0 + 0.05 * jax.random.normal(k, shape, f32)

    b_in = 0.02 * jax.random.normal(ks[7], (DEPTH, IN_WIDTH), f32)
    f_bias = jnp.linspace(3.0, 6.0, MLSTM_HEADS, dtype=f32)
    b_in = b_in.at[:, F_OFFSET:F_OFFSET + SZ_MF].add(f_bias)
    return {
        "x": jax.random.normal(ks[0], (BATCH, SEQ, D_MODEL), f32),
        "ffn1_norm": gain(ks[1], (DEPTH, D_MODEL)),
        "ffn1_w_gate": dense(ks[2], D_MODEL, D_FF),
        "ffn1_w_up": dense(ks[3], D_MODEL, D_FF),
        "ffn1_w_down": dense(ks[4], D_FF, D_MODEL),
        "mix_norm": gain(ks[5], (DEPTH, D_MODEL)),
        "w_in": dense(ks[6], D_MODEL, IN_WIDTH),
        "b_in": b_in,
        "attn_sinks": 0.5 * jax.random.normal(ks[8], (DEPTH, ATTN_HEADS), f32),
        "mlstm_conv": jax.random.normal(ks[9], (DEPTH, CONV_WIDTH, SZ_MQ + SZ_MK), f32) * CONV_WIDTH ** -0.5,
        "mlstm_head_norm": gain(ks[10], (DEPTH, MLSTM_HEADS, MLSTM_V_DIM)),
        "w_proj_attn": dense(ks[11], SZ_AQ, D_MODEL),
        "w_proj_mlstm": dense(ks[12], SZ_MV, D_MODEL),
        "w_out": dense(ks[13], D_MODEL, D_MODEL),
        "ffn2_norm": gain(ks[14], (DEPTH, D_MODEL)),
        "ffn2_w_gate": dense(ks[15], D_MODEL, D_FF),
        "ffn2_w_up": dense(ks[16], D_MODEL, D_FF),
        "ffn2_w_down": dense(ks[17], D_FF, D_MODEL),
        "final_norm": gain(ks[18], (D_MODEL,)),
    }


def reference(x, ffn1_norm, ffn1_w_gate, ffn1_w_up, ffn1_w_down, mix_norm, w_in, b_in,
              attn_sinks, mlstm_conv, mlstm_head_norm, w_proj_attn, w_proj_mlstm, w_out,
              ffn2_norm, ffn2_w_gate, ffn2_w_up, ffn2_w_down, final_norm):
    b, s, _ = x.shape
    for l in range(DEPTH):
        h = rms_norm(x, ffn1_norm[l])
        x = x + FFN_RESIDUAL_WEIGHT * swiglu(h, ffn1_w_gate[l], ffn1_w_up[l], ffn1_w_down[l])

        h = rms_norm(x, mix_norm[l])
        z = h @ w_in[l] + b_in[l]
        a_q, a_k, a_v, m_q, m_k, m_v, m_o, m_i, m_f, g_pre = jnp.split(z, SPLIT_POINTS, axis=-1)

        y_attn = sliding_window_attention(
            a_q.reshape(b, s, ATTN_HEADS, ATTN_HEAD_DIM),
            a_k.reshape(b, s, ATTN_KV_HEADS, ATTN_HEAD_DIM),
            a_v.reshape(b, s, ATTN_KV_HEADS, ATTN_HEAD_DIM),
            attn_sinks[l]) @ w_proj_attn[l]

        qk = jax.nn.silu(causal_depthwise_conv(jnp.concatenate([m_q, m_k], axis=-1), mlstm_conv[l]))
        mq = qk[..., :SZ_MQ].reshape(b, s, MLSTM_HEADS, MLSTM_QK_DIM)
        mk = qk[..., SZ_MQ:].reshape(b, s, MLSTM_HEADS, MLSTM_QK_DIM) * MLSTM_QK_DIM ** -0.5
        mv = m_v.reshape(b, s, MLSTM_HEADS, MLSTM_V_DIM)
        hm = mlstm_chunkwise(mq, mk, mv,
                             softcap(m_i.astype(jnp.float32)), softcap(m_f.astype(jnp.float32)))
        hm = rms_norm(hm, mlstm_head_norm[l]).reshape(b, s, SZ_MV)
        hm = (jax.nn.sigmoid(m_o.astype(jnp.float32)) * hm).astype(x.dtype)
        y_mlstm = hm @ w_proj_mlstm[l]

        gates = jax.nn.sigmoid(g_pre.astype(jnp.float32)).astype(x.dtype).reshape(b, s, 2, D_MODEL)
        merged = gates[:, :, 0] * y_attn + gates[:, :, 1] * y_mlstm
        x = x + merged @ w_out[l]

        h = rms_norm(x, ffn2_norm[l])
        x = x + FFN_RESIDUAL_WEIGHT * swiglu(h, ffn2_w_gate[l], ffn2_w_up[l], ffn2_w_down[l])
    return rms_norm(x, final_norm)
```

```python
import contextlib
import numpy as np
import concourse.bass as bass
import concourse.mybir as mybir
from concourse.bass_utils import run_bass_kernel_spmd

F32 = mybir.dt.float32
BF16 = mybir.dt.bfloat16
AF = mybir.ActivationFunctionType
ALU = mybir.AluOpType
ENGS = ("pe", "act", "dve", "pool", "sp")

D_MODEL = 1024
D_FF = 2816
NMF = D_FF // 128
AQ, AK, AV, MQ, MK, MV, MO, MI, MF, GP = 0, 1024, 1280, 1536, 2048, 2560, 3584, 4608, 4616, 4624
NFM = 56
FM_AQ, FM_AK, FM_MO, FM_F, FM_I, FM_MQ, FM_MK, FM_G = 0, 8, 16, 24, 28, 32, 36, 40
ZC = 6672
C_BFM, C_GAIN, C_HN, C_CONV, C_SINK, C_FLAG, C_BTM, C_SCAN, C_EPS, C_ONE, C_LN8, C_TINY = (
    0, 56, 88, 96, 128, 144, 145, 1681, 2193, 2194, 2195, 2196)
NCF = 2197
import os
ATT_SUB = int(os.environ.get('ATT_SUB', '9'))


class Op:
    __slots__ = ("eng", "fn", "reads", "writes", "dma", "deps", "signal", "cnt", "semkey", "idx", "epoch", "eseq")


class Prog:
    def __init__(self, nc):
        self.nc = nc
        self.ops = []
        self.last_w = {}
        self.readers = {}
        self.epoch = 0
        self.eseq = {e: 0 for e in ENGS}

    def op(self, eng, fn, reads=(), writes=(), dma=False, semkey=None):
        o = Op()
        o.eng, o.fn, o.reads, o.writes, o.dma, o.semkey = eng, fn, tuple(reads), tuple(writes), dma, semkey
        o.signal, o.cnt, o.epoch = False, 0, self.epoch
        o.idx = len(self.ops)
        o.eseq = self.eseq[eng]
        self.eseq[eng] += 1
        raw = set()
        deps = set()
        for k in o.reads:
            w = self.last_w.get(k)
            if w is not None:
                deps.add(w)
                raw.add(w)
        for k in o.writes:
            w = self.last_w.get(k)
            if w is not None:
                deps.add(w)
            deps.update(self.readers.get(k, {}).values())
        need = []
        for d in deps:
            dop = self.ops[d]
            if dop.dma or dop.eng != eng:
                need.append(d)
            elif eng != "pe" and not dma:
                if d in raw or (o.eseq - dop.eseq) <= 3:
                    need.append(d)
        o.deps = need
        for d in need:
            self.ops[d].signal = True
        for k in o.reads:
            rd = self.readers.setdefault(k, {})
            rd[("dma", o.idx) if dma else eng] = o.idx
        for k in o.writes:
            self.last_w[k] = o.idx
            self.readers[k] = {}
        self.ops.append(o)
        return o

    def dma(self, q, out, in_, reads=(), writes=(), semkey=None):
        return self.op(q, lambda e: e.dma_start(out=out, in_=in_), reads, writes, dma=True, semkey=semkey)

    def emit(self, final_wait_keys=()):
        nc = self.nc
        eng_cnt = {}
        dma_cnt = {}
        for o in self.ops:
            if o.dma:
                dma_cnt[o.semkey] = dma_cnt.get(o.semkey, 0) + 16
                o.cnt = dma_cnt[o.semkey]
            elif o.signal:
                ek = (o.eng, o.epoch)
                eng_cnt[ek] = eng_cnt.get(ek, 0) + 1
                o.cnt = eng_cnt[ek]
        semkeys = sorted(dma_cnt.keys(), key=str)
        with contextlib.ExitStack() as st:
            esem = {ek: st.enter_context(nc.semaphore("s_%s_%d" % ek)) for ek in sorted(eng_cnt.keys())}
            dsem = {k: st.enter_context(nc.semaphore("d_%d" % i)) for i, k in enumerate(semkeys)}
            block = st.enter_context(nc.Block())
            ops = self.ops

            def run(engname, eng):
                waited = {}
                for o in ops:
                    if o.eng != engname:
                        continue
                    for d in o.deps:
                        dop = ops[d]
                        if dop.dma:
                            sem, val, key = dsem[dop.semkey], dop.cnt, ("d", dop.semkey)
                        else:
                            ek = (dop.eng, dop.epoch)
                            sem, val, key = esem[ek], dop.cnt, ("e", ek)
                        if waited.get(key, 0) >= val:
                            continue
                        waited[key] = val
                        eng.wait_ge(sem, val)
                    ins = o.fn(eng)
                    if o.dma:
                        ins.then_inc(dsem[o.semkey], 16)
                    elif o.signal:
                        ins.then_inc(esem[(o.eng, o.epoch)], 1)
                if engname == "sp":
                    for k in final_wait_keys:
                        eng.wait_ge(dsem[k], dma_cnt[k])

            block.tensor(lambda e: run("pe", e))
            block.scalar(lambda e: run("act", e))
            block.vector(lambda e: run("dve", e))
            block.gpsimd(lambda e: run("pool", e))
            block.sync(lambda e: run("sp", e))


def build_program(n_tg_prev=4, n_tg_own=4, dbg=None, level=9):
    nc = bass.Bass("TRN2", target_bir_lowering=False)

    def DI(name, shape):
        return nc.dram_tensor(name, shape, F32, kind="ExternalInput").ap()

    xin = DI("xT_in", [8, 128, 4096])
    cf_d = DI("cf", [128, NCF])
    cb_d = DI("cb", [128, 512])
    wffn = {1: (DI("wg1", [NMF, 128, 1024]), DI("wu1", [NMF, 128, 1024]), DI("wd1", [NMF, 128, 1024])),
            2: (DI("wg2", [NMF, 128, 1024]), DI("wu2", [NMF, 128, 1024]), DI("wd2", [NMF, 128, 1024]))}
    wfm_d = DI("wfm", [NFM, 128, 1024])
    wtm_d = DI("wtm", [6, 128, 2048])
    wpa_d = DI("wpa", [8, 128, 1024])
    wpm_d = DI("wpm", [8, 128, 1024])
    wo_d = DI("wo", [8, 128, 1024])
    out_d = nc.dram_tensor("outT", [8, 128, 2048], F32, kind="ExternalOutput").ap()
    dbg_d = {}
    if dbg:
        for name, shape in dbg.items():
            dbg_d[name] = nc.dram_tensor("dbg_" + name, shape, F32, kind="ExternalOutput").ap()

    with contextlib.ExitStack() as st:
        def SB(name, shape, dt):
            return st.enter_context(nc.sbuf_tensor("sb_" + name, shape, dt))

        P = Prog(nc)
        xbuf = [SB("xT%d" % i, [128, 8, 512], F32) for i in range(2)]
        X = {"t": xbuf[0], "i": 0}

        def xk(kc):
            return ("x", X["i"], kc)

        def xall():
            return [("x", X["i"], kc) for kc in range(8)]
        hT = SB("hT", [128, 8, 512], BF16)
        MG = 4
        aT = SB("aT", [128, MG, 512], BF16)
        cf = SB("cfs", [128, NCF], F32)
        cb = SB("cbs", [128, 512], BF16)
        aqT = SB("aqT", [128, 8, 512], BF16)
        ak2T = SB("ak2T", [128, 8, 640], BF16)
        kz = SB("kz", [128, 8, 512], BF16)
        av2 = SB("av2", [128, 5, 512], BF16)
        moT = SB("moT", [128, 8, 512], BF16)
        qkT = SB("qkT", [128, 4, 512], BF16)
        mv = SB("mv", [128, 4, 8, 130], BF16)
        yaT = SB("yaT", [128, 8, 512], BF16)
        hmT = SB("hmT", [128, 8, 512], BF16)
        mgT = aqT
        Cst = SB("Cst", [128, 4, 129], F32)
        Cb = SB("Cb", [128, 2, 8, 128], BF16)
        ksum = SB("ksum", [128, 2, 8], F32)
        Nb = SB("Nb", [128, 2, 8, 128], BF16)
        eg = SB("eg", [128, 4, 4], F32)
        halo = SB("halo", [128, 8, 3], F32)
        b15 = SB("b15", [128, 8], F32)
        esink = SB("esink", [128, 16], F32)
        mpf = SB("mpf", [128, 128], BF16)
        mbp = SB("mbp", [128, 512], BF16)
        mbc = SB("mbc", [128, 512], BF16)
        mbf = SB("mbf", [128, 512], BF16)
        esrow = SB("esrow", [1, 2048], BF16)
        pre = [SB("pre%d" % i, [128, 515], F32) for i in range(2)]
        rings = {
            "w": [SB("w%d" % i, [128, 1024], BF16) for i in range(8)],
            "wd": [SB("wd%d" % i, [128, 1024], BF16) for i in range(2 * MG)],
            "tf": [SB("tf%d" % i, [128, 512], F32) for i in range(8)],
            "tb": [SB("tb%d" % i, [128, 512], BF16) for i in range(8)],
        }
        rpos = {k: 0 for k in rings}
        psum = st.enter_context(nc.psum_tensor("psum", [128, 4096], F32))
        pstate = {"i": 0}

        live_banks = set()

        def nb():
            for _ in range(7):
                b = pstate["i"] % 7
                pstate["i"] += 1
                if b not in live_banks:
                    live_banks.add(b)
                    return b
            raise AssertionError("all PSUM banks live")

        _orig_op = P.op

        def _op(eng, fn, reads=(), writes=(), dma=False, semkey=None):
            for k in reads:
                if isinstance(k, tuple) and k[0] == "ps":
                    live_banks.discard(k[1])
            return _orig_op(eng, fn, reads, writes, dma, semkey)

        P.op = _op

        def ps(b):
            return psum[:, b * 512:(b + 1) * 512]

        def pk(b):
            return ("ps", b)

        def rnext(name):
            i = rpos[name] % len(rings[name])
            rpos[name] += 1
            return rings[name][i], (name, i)

        def tf():
            t, k = rnext("tf")
            return t[:], k

        def tb():
            t, k = rnext("tb")
            return t[:], k

        def ring_load(name, src):
            t, k = rnext(name)
            P.dma("pool", t[:], src, writes=[k], semkey=k)
            return t, k

        def MM(out, lhsT, rhs, start, stop, reads, writes):
            P.op("pe", lambda e: e.matmul(out, lhsT=lhsT, rhs=rhs, start=start, stop=stop), reads, writes)

        def ACT(out, in_, func, reads, writes, bias=None, scale=1.0):
            if bias is None:
                P.op("act", lambda e: e.activation(out=out, in_=in_, func=func, scale=scale), reads, writes)
            else:
                P.op("act", lambda e: e.activation(out=out, in_=in_, func=func, bias=bias, scale=scale), reads, writes)

        def TT(out, in0, in1, op, reads, writes, eng="dve"):
            P.op(eng, lambda e: e.tensor_tensor(out=out, in0=in0, in1=in1, op=op), reads, writes)

        def TS(out, in0, s1, op0, reads, writes, eng="dve"):
            P.op(eng, lambda e: e.tensor_scalar(out=out, in0=in0, scalar1=s1, scalar2=None, op0=op0), reads, writes)

        def STT(out, in0, scalar, in1, op0, op1, reads, writes, eng="dve"):
            P.op(eng, lambda e: e.scalar_tensor_tensor(out=out, in0=in0, scalar=scalar, in1=in1, op0=op0, op1=op1),
                 reads, writes)

        def TC(out, in_, reads, writes, eng="dve"):
            P.op(eng, lambda e: e.tensor_copy(out=out, in_=in_), reads, writes)

        def RCP(out, in_, reads, writes):
            P.op("dve", lambda e: e.reciprocal(out=out, in_=in_), reads, writes)

        def r4(ap):
            return ap.rearrange("p (a b) -> p a b", a=4)

        def r2(ap):
            return ap.rearrange("p (a b) -> p a b", a=2)

        P.dma("sp", cf[:], cf_d, writes=["cf"], semkey="cf")
        P.dma("pool", cb[:], cb_d, writes=["cb"], semkey="cb")
        mask_cur, mask_prev, ident, ones = cb[:, 0:128], cb[:, 128:256], cb[:, 256:384], cb[:, 384:512]
        flag = cf[:, C_FLAG:C_FLAG + 1]
        eps = cf[:, C_EPS:C_EPS + 1]
        one = cf[:, C_ONE:C_ONE + 1]
        ln8 = cf[:, C_LN8:C_LN8 + 1]
        tiny = cf[:, C_TINY:C_TINY + 1]
        scanmask = cf[:, C_SCAN:C_SCAN + 512]
        btm = cf[:, C_BTM:C_BTM + 1536]

        def bfm(ci):
            return cf[:, C_BFM + ci:C_BFM + ci + 1]

        TS(b15[:], cf[:, C_BFM + FM_F:C_BFM + FM_F + 8], 1.0 / 15.0, ALU.mult, ["cf"], ["b15"])
        ACT(esink[:], cf[:, C_SINK:C_SINK + 16], AF.Exp, ["cf"], ["esink"])
        TS(mpf[:], mask_prev, flag, ALU.mult, ["cb", "cf"], ["mpf"])
        for (dst, src, k_) in ((mbp, mask_prev, "cb"), (mbc, mask_cur, "cb"), (mbf, mpf[:], "mpf")):
            P.op("dve", lambda e, dst=dst, src=src: e.tensor_scalar(
                out=r4(dst[:]), in0=src.unsqueeze(1).to_broadcast([128, 4, 128]), scalar1=-1.0, scalar2=30000.0,
                op0=ALU.add, op1=ALU.mult), [k_], ["mbp" if dst is not mbf else "mbf"])
        TC(esrow[0:1, :].rearrange("p (a b) -> p a b", a=16), esink[0:1, :].unsqueeze(2).to_broadcast([1, 16, 128]),
           ["esink"], ["esrow"])
        P.op("dve", lambda e: e.memset(halo[:], 0.0), writes=["halo"])
        P.op("dve", lambda e: e.memset(Cst[:], 0.0), writes=["C"])
        P.op("dve", lambda e: e.memset(Cb[:], 0.0), writes=[("Cb", 0), ("Cb", 1)])
        P.op("dve", lambda e: e.memset(Nb[:], 0.0), writes=[("Cb", 0), ("Cb", 1)])
        P.op("dve", lambda e: e.memset(mv[:], 1.0), writes=["mv"])
        P.op("dve", lambda e: e.memset(ak2T[:], 0.0), writes=["ak2T"])
        P.op("dve", lambda e: e.memset(kz[:], 0.0), writes=["kz"])
        P.op("dve", lambda e: e.memset(av2[:], 0.0), writes=["av2"])

        class NormAcc:
            def __init__(self):
                self.b = 7
                self.n = 0

            def add(self, kc):
                sq, sqk = tb()
                ACT(sq, X["t"][:, kc, :], AF.Square, [xk(kc)], [sqk])
                MM(ps(self.b), ones, sq, self.n == 0, self.n == 7, [sqk, "cb"], [pk(self.b)])
                self.n += 1

            def finish(self, gi, dst, dkey):
                assert self.n == 8
                b = self.b
                rs, rsk = tf()
                ACT(rs, ps(b), AF.Ln, [pk(b), "cf"], [rsk], bias=eps, scale=1.0 / D_MODEL)
                ACT(rs, rs, AF.Exp, [rsk], [rsk], scale=-0.5)
                for kc in range(8):
                    g = cf[:, C_GAIN + gi * 8 + kc:C_GAIN + gi * 8 + kc + 1]
                    dk = (dkey, kc) if dkey == "hT" else xk(kc)
                    d_ = dst if dkey == "hT" else X["t"]
                    STT(d_[:, kc, :], X["t"][:, kc, :], g, rs, ALU.mult, ALU.mult, [xk(kc), rsk, "cf"], [dk])

        def norm(gi, dst, dkey):
            na = NormAcc()
            for kc in range(8):
                na.add(kc)
            na.finish(gi, dst, dkey)

        def ffn(which, hook=None):
            wg_d, wu_d, wd_d = wffn[which]
            m0 = 0
            while m0 < NMF:
                grp = list(range(m0, min(m0 + MG, NMF)))
                m0 += MG
                wds = []
                for j, m in enumerate(grp):
                    wg_t, wg_k = ring_load("w", wg_d[m])
                    wu_t, wu_k = ring_load("w", wu_d[m])
                    wds.append(ring_load("wd", wd_d[m]))
                    bg, bu = nb(), nb()
                    for kc in range(8):
                        MM(ps(bg), wg_t[:, kc * 128:(kc + 1) * 128], hT[:, kc, :], kc == 0, kc == 7,
                           [wg_k, ("hT", kc)], [pk(bg)])
                    for kc in range(8):
                        MM(ps(bu), wu_t[:, kc * 128:(kc + 1) * 128], hT[:, kc, :], kc == 0, kc == 7,
                           [wu_k, ("hT", kc)], [pk(bu)])
                    s, sk = tf()
                    ACT(s, ps(bg), AF.Silu, [pk(bg)], [sk])
                    TT(aT[:, j, :], s, ps(bu), ALU.mult, [sk, pk(bu)], [("aT", j)])
                for n in range(8):
                    by = nb()
                    for j in range(len(grp)):
                        wd_t, wd_k = wds[j]
                        MM(ps(by), wd_t[:, n * 128:(n + 1) * 128], aT[:, j, :], j == 0, j == len(grp) - 1,
                           [wd_k, ("aT", j)], [pk(by)])
                    STT(X["t"][:, n, :], ps(by), 0.5, X["t"][:, n, :], ALU.mult, ALU.add, [pk(by), xk(n)], [xk(n)])
                    if hook is not None and m0 >= NMF:
                        if n >= 2:
                            hook(n - 2)
                        if n == 7:
                            hook(6)
                            hook(7)

        def fm_mm(src, rhs_t, rhs_key):
            wt, wk = ring_load("w", src)
            b = nb()
            for kc in range(8):
                rk_ = (rhs_key, kc) if rhs_key == "hT" else rhs_key
                MM(ps(b), wt[:, kc * 128:(kc + 1) * 128], rhs_t[:, kc, :], kc == 0, kc == 7, [wk, rk_], [pk(b)])
            return b

        def w_in(t, own):
            first_own = (t == n_tg_prev)
            need_halo = own or (t == n_tg_prev - 1)
            tm_todo = [gi for gi in range(6) if (gi >= 2 or need_halo)]

            def tm_group():
                if not tm_todo:
                    return
                gi = tm_todo.pop(0)
                halves = [ring_load("w", wtm_d[gi][:, hf_ * 1024:(hf_ + 1) * 1024]) for hf_ in range(2)]
                for blk in range(4):
                    b = nb()
                    for kc in range(8):
                        wt, wk = halves[kc // 4]
                        MM(ps(b)[:, 0:256], hT[:, kc, blk * 128:(blk + 1) * 128], wt[:, (kc % 4) * 256:(kc % 4 + 1) * 256],
                           kc == 0, kc == 7, [wk, ("hT", kc)], [pk(b)])
                    bias = btm[:, gi * 256:(gi + 1) * 256]
                    if gi < 2:
                        TT(av2[:, 1 + blk, gi * 256:(gi + 1) * 256], ps(b)[:, 0:256], bias, ALU.add, [pk(b), "cf"], ["av2"])
                    else:
                        h0 = (gi - 2) * 2
                        TT(mv[:, blk, h0:h0 + 2, 0:128], r2(ps(b)[:, 0:256]), r2(bias), ALU.add, [pk(b), "cf"], ["mv"])

            if own:
                for c in range(8):
                    b = fm_mm(wfm_d[c], hT, "hT")
                    ACT(aqT[:, c, :], ps(b), AF.Identity, [pk(b), "cf"], ["aqT"], bias=bfm(c))
                    if c == 3:
                        tm_group()
            if need_halo:
                tm_group()
            for g in range(8):
                if not need_halo:
                    break
                b = fm_mm(wfm_d[FM_AK + g], hT, "hT")
                ACT(ak2T[:, g, 128:640], ps(b), AF.Identity, [pk(b), "cf"], ["ak2T"], bias=bfm(FM_AK + g))
            if own:
                for h in range(8):
                    b = fm_mm(wfm_d[FM_MO + h], hT, "hT")
                    s, sk = tf()
                    ACT(s, ps(b), AF.Sigmoid, [pk(b), "cf"], [sk], bias=bfm(FM_MO + h))
                    TS(moT[:, h, :], s, cf[:, C_HN + h:C_HN + h + 1], ALU.mult, [sk, "cf"], ["moT"])
            need_q = own or (t == n_tg_prev - 1)
            for c in range(4):
                tm_group()
                b_f = fm_mm(wfm_d[FM_F + c], hT, "hT")
                b_i = fm_mm(wfm_d[FM_I + c], hT, "hT")
                chunks = []
                for (ci, qc) in ((FM_MQ + c, c), (FM_MK + c, 4 + c)):
                    if qc < 4 and not need_q:
                        continue
                    chunks.append((ci, qc, fm_mm(wfm_d[ci], hT, "hT")))
                t1, t1k = tf()
                t4, t4k = tf()
                bt, btk = tf()
                eb, ebk = tf()
                ek, ekk = tf()
                ACT(t1, ps(b_f), AF.Tanh, [pk(b_f), "b15"], [t1k], bias=b15[:, c:c + 1], scale=1.0 / 15.0)
                ACT(t4, ps(b_i), AF.Tanh, [pk(b_i), "b15"], [t4k], bias=b15[:, 4 + c:5 + c], scale=1.0 / 15.0)
                pres = []
                for k_, (ci, qc, b) in enumerate(chunks):
                    pi = rpos.setdefault("pre", 0) % 2
                    rpos["pre"] += 1
                    pt, pkk = pre[pi], ("pre", pi)
                    ACT(pt[:, 3:515], ps(b), AF.Identity, [pk(b), "cf"], [pkk], bias=bfm(ci))
                    pres.append((pt, pkk))
                    if k_ == 0:
                        ACT(t1, t1, AF.Exp, [t1k], [t1k], scale=-15.0)
                if not chunks:
                    ACT(t1, t1, AF.Exp, [t1k], [t1k], scale=-15.0)
                ACT(t1, t1, AF.Ln, [t1k, "cf"], [t1k], bias=one)
                accs = []
                for k_, (ci, qc, b) in enumerate(chunks):
                    pt, pkk = pres[k_]
                    if first_own:
                        TS(pt[:, 0:3], halo[:, qc, :], flag, ALU.mult, ["halo", "cf"], [pkk])
                    else:
                        TC(pt[:, 0:3], halo[:, qc, :], ["halo"], [pkk])
                    acc, ack = tf()
                    cw = C_CONV + qc * 4
                    TS(acc, pt[:, 0:512], cf[:, cw:cw + 1], ALU.mult, [pkk, "cf"], [ack])
                    for j in range(1, 4):
                        STT(acc, pt[:, j:j + 512], cf[:, cw + j:cw + j + 1], acc, ALU.mult, ALU.add,
                            [pkk, "cf", ack], [ack])
                    TC(halo[:, qc, :], pt[:, 512:515], [pkk], ["halo"])
                    accs.append((acc, ack, qc))
                    if k_ == 0:
                        P.op("dve", lambda e, bt=bt, t1=t1: e.tensor_tensor_scan(
                            out=bt, data0=scanmask, data1=t1, initial=0.0, op0=ALU.mult, op1=ALU.subtract),
                            [t1k, "cf"], [btk])
                if not chunks:
                    raise AssertionError
                if len(accs) >= 1:
                    ACT(accs[0][0], accs[0][0], AF.Silu, [accs[0][1]], [accs[0][1]])
                ACT(eb, bt, AF.Exp, [btk], [ebk])
                STT(t4, t4, 15.0, bt, ALU.mult, ALU.subtract, [t4k, btk], [t4k])
                if len(accs) >= 2:
                    ACT(accs[1][0], accs[1][0], AF.Silu, [accs[1][1]], [accs[1][1]])
                ACT(ek, t4, AF.Exp, [t4k, "cf"], [ekk], bias=ln8)
                TC(eg[:, c, :], r4(eb)[:, :, 127], [ebk], ["eg"])
                for (acc, ack, qc) in accs:
                    if qc < 4:
                        TT(qkT[:, qc, :], acc, eb, ALU.mult, [ack, ebk], ["qkT"])
                    else:
                        TT(kz[0:64, 2 * c, :], acc[0:64, :], ek[0:64, :], ALU.mult, [ack, ekk], ["kz"])
                        TT(kz[64:128, 2 * c + 1, :], acc[64:128, :], ek[64:128, :], ALU.mult, [ack, ekk], ["kz"])
            while tm_todo:
                tm_group()
        def attention(t):
            first_own = (t == n_tg_prev)
            items = [(blk, g) for blk in range(4) for g in range(4)]
            Ems = {}

            def stage_a(blk, g):
                qs = slice(blk * 128, (blk + 1) * 128)
                Em = []
                for kb in range(2):
                    b = nb()
                    ks = slice((blk + kb) * 128, (blk + kb + 1) * 128)
                    if kb == 0:
                        mb_, mk_ = (mbf[:], "mbf") if (first_own and blk == 0) else (mbp[:], "mbp")
                    else:
                        mb_, mk_ = mbc[:], "mbp"
                    MM(ps(b), ident, mb_, True, False, ["cb", mk_], [pk(b)])
                    for bi in range(2):
                        for cc in range(2):
                            c0 = bi * 256 + cc * 128
                            MM(ps(b)[:, c0:c0 + 128], ak2T[:, 2 * g + bi, ks], aqT[:, 2 * g + cc, qs],
                               False, (bi == 1 and cc == 1), ["ak2T", "aqT"], [pk(b)])
                    E, Ek = tb()
                    ACT(E, ps(b), AF.Exp, [pk(b)], [Ek], scale=0.125)
                    Em.append((E, Ek))
                Ems[(blk, g)] = Em

            def stage_b(blk, g):
                qs = slice(blk * 128, (blk + 1) * 128)
                Em = Ems.pop((blk, g))
                bo, bd = nb(), nb()
                for kb in range(2):
                    MM(ps(bo), av2[:, blk + kb, g * 128:(g + 1) * 128], Em[kb][0], kb == 0, kb == 1,
                       ["av2", Em[kb][1]], [pk(bo)])
                for kb in range(2):
                    MM(ps(bd), ones, Em[kb][0], kb == 0, False, ["cb", Em[kb][1]], [pk(bd)])
                MM(ps(bd), ones[0:1, :], esrow[0:1, g * 512:(g + 1) * 512], False, True, ["cb", "esrow"], [pk(bd)])
                r, rk = tf()
                ACT(r, ps(bd), AF.Ln, [pk(bd)], [rk])
                ACT(r, r, AF.Exp, [rk], [rk], scale=-1.0)
                for bi in range(2):
                    rs_ = slice(bi * 64, bi * 64 + 64)
                    cs = slice(bi * 256, (bi + 1) * 256)
                    TT(yaT[rs_, 2 * g:2 * g + 2, qs], r2(ps(bo)[rs_, cs]), r2(r[rs_, cs]), ALU.mult,
                       [pk(bo), rk], ["yaT"])

            stage_a(*items[0])
            for i, it in enumerate(items):
                if i + 1 < len(items):
                    stage_a(*items[i + 1])
                stage_b(*it)

        def halo_copy():
            TC(ak2T[:, :, 0:128], ak2T[:, :, 512:640], ["ak2T"], ["ak2T"])
            TC(av2[:, 0, :], av2[:, 4, :], ["av2"], ["av2"])

        CbV = [[Cb[hh_ * 64:(hh_ + 1) * 64, v].rearrange("p (c h) w -> p c h w", h=2)[:, :, hh_, :] for hh_ in range(2)]
               for v in range(2)]
        NbV = [[Nb[hh_ * 64:(hh_ + 1) * 64, v].rearrange("p (c h) w -> p c h w", h=2)[:, :, hh_, :] for hh_ in range(2)]
               for v in range(2)]
        mst = {"g": 0, "ks": 0}

        def recast_state(v):
            for hh_ in range(2):
                hs = slice(hh_ * 64, hh_ * 64 + 64)
                TC(CbV[v][hh_], Cst[hs, :, 0:128], ["C"], [("Cb", v)])
                TC(NbV[v][hh_], Cst[hs, :, 128:129].to_broadcast([64, 4, 128]), ["C"], [("Cb", v)])

        def mlstm(t, own):
            if t == n_tg_prev:
                TS(Cst[:], Cst[:], flag, ALU.mult, ["C", "cf"], ["C"])
                recast_state(mst["g"] % 2)
            g0 = mst["g"]
            mst["g"] += 4
            Ub, Sm, post = {}, {}, {}

            def tu_mm(blk):
                qs = slice(blk * 128, (blk + 1) * 128)
                bK = nb()
                for c in range(4):
                    for hh_ in range(2):
                        MM(ps(bK)[:, c * 128:(c + 1) * 128], kz[:, 2 * c + hh_, qs], ident, hh_ == 0, hh_ == 1,
                           ["kz", "cb"], [pk(bK)])
                kt, ktk = tb()
                ACT(kt, ps(bK), AF.Identity, [pk(bK)], [ktk])
                bU = [nb(), nb()]
                for c in range(4):
                    MM(r2(ps(bU[c // 2])[:, (c % 2) * 256:(c % 2) * 256 + 256]), kt[:, c * 128:(c + 1) * 128],
                       mv[:, blk, 2 * c:2 * c + 2, 0:128], True, True, [ktk, "mv"], [pk(bU[c // 2])])
                Ub[blk] = bU

            def upd(blk):
                qs = slice(blk * 128, (blk + 1) * 128)
                bU = Ub.pop(blk)
                ki = mst["ks"] % 2
                mst["ks"] += 1
                P.op("dve", lambda e: e.reduce_sum(out=ksum[:, ki, :], in_=kz[:, :, qs], axis=mybir.AxisListType.X),
                     ["kz"], [("ks", ki)])
                for b_ in range(2):
                    for hh_ in range(2):
                        hs = slice(hh_ * 64, hh_ * 64 + 64)
                        uv = ps(bU[b_])[hs, :].rearrange("p (c h w) -> p c h w", c=2, h=2)[:, :, hh_, :]
                        TT(Cst[hs, 2 * b_:2 * b_ + 2, 0:128], uv, Cst[hs, 2 * b_:2 * b_ + 2, 0:128], ALU.add,
                           [pk(bU[b_]), "C"], ["C"])
                ks2 = ksum[:, ki, :].rearrange("p (c h) -> p c h", h=2)
                TT(Cst[:, :, 128], Cst[:, :, 128], ks2[:, :, 0], ALU.add, ["C", ("ks", ki)], ["C"])
                TT(Cst[:, :, 128], Cst[:, :, 128], ks2[:, :, 1], ALU.add, ["C", ("ks", ki)], ["C"])
                TT(Cst[:], Cst[:], eg[:, :, blk:blk + 1].to_broadcast([128, 4, 129]), ALU.mult, ["C", "eg"], ["C"])
                recast_state((g0 + blk + 1) % 2)

            def s_stage(blk):
                qs = slice(blk * 128, (blk + 1) * 128)
                bS = [nb(), nb()]
                for h in range(8):
                    c = h // 2
                    cs = slice((h % 4) * 128, (h % 4) * 128 + 128)
                    MM(ps(bS[h // 4])[:, cs], kz[:, h, qs], qkT[:, c, qs], True, True, ["qkT", "kz"], [pk(bS[h // 4])])
                sm = []
                for i in range(2):
                    s_, sk_ = tb()
                    TT(r4(s_), r4(ps(bS[i])), mask_cur.unsqueeze(1).to_broadcast([128, 4, 128]), ALU.mult,
                       [pk(bS[i]), "cb"], [sk_])
                    sm.append((s_, sk_))
                Sm[blk] = sm

            def nd_stage(blk):
                qs = slice(blk * 128, (blk + 1) * 128)
                v = (g0 + blk) % 2
                sm = Sm.pop(blk)
                bN = [nb(), nb()]
                bD = [nb(), nb()]
                for h in range(8):
                    c = h // 2
                    cs = slice((h % 4) * 128, (h % 4) * 128 + 128)
                    i = h // 4
                    MM(ps(bN[i])[:, cs], mv[:, blk, h, 0:128], sm[i][0][:, cs], True, False, ["mv", sm[i][1]], [pk(bN[i])])
                    MM(ps(bN[i])[:, cs], Cb[:, v, h, :], qkT[:, c, qs], False, True, [("Cb", v), "qkT"], [pk(bN[i])])
                    MM(ps(bD[i])[:, cs], ones, sm[i][0][:, cs], True, False, ["cb", sm[i][1]], [pk(bD[i])])
                    MM(ps(bD[i])[:, cs], Nb[:, v, h, :], qkT[:, c, qs], False, True, [("Cb", v), "qkT"], [pk(bD[i])])
                pp = []
                for i in range(2):
                    a, ak = tf()
                    ACT(a, ps(bD[i]), AF.Abs, [pk(bD[i])], [ak])
                    ACT(a, a, AF.Ln, [ak, "cf"], [ak], bias=tiny)
                    ACT(a, a, AF.Exp, [ak], [ak], scale=-1.0)
                    hh, hk = tf()
                    STT(hh, a, 1.0, ps(bN[i]), ALU.min, ALU.mult, [ak, pk(bN[i])], [hk])
                    sq, sqk = tb()
                    ACT(sq, hh, AF.Square, [hk], [sqk])
                    pp.append((hh, hk, sq, sqk))
                post[blk] = pp

            def ssq_stage(blk):
                qs = slice(blk * 128, (blk + 1) * 128)
                for i, (hh, hk, sq, sqk) in enumerate(post.pop(blk)):
                    bq = nb()
                    MM(ps(bq), ones, sq, True, True, ["cb", sqk], [pk(bq)])
                    rs, rsk = tf()
                    ACT(rs, ps(bq), AF.Ln, [pk(bq), "cf"], [rsk], bias=eps, scale=1.0 / 128.0)
                    ACT(rs, rs, AF.Exp, [rsk], [rsk], scale=-0.5)
                    TT(hh, hh, rs, ALU.mult, [hk, rsk], [hk])
                    TT(hmT[:, 4 * i:4 * i + 4, qs], r4(hh), moT[:, 4 * i:4 * i + 4, qs], ALU.mult,
                       [hk, "moT"], ["hmT"])

            if not own:
                for blk in range(4):
                    tu_mm(blk)
                    upd(blk)
                return
            tu_mm(0); upd(0); s_stage(0)
            tu_mm(1); s_stage(1)
            nd_stage(0); upd(1)
            tu_mm(2); s_stage(2)
            ssq_stage(0)
            nd_stage(1); upd(2)
            tu_mm(3); s_stage(3)
            ssq_stage(1)
            nd_stage(2); upd(3)
            ssq_stage(2)
            nd_stage(3)
            ssq_stage(3)

        def proj(hook=None):
            for n in range(8):
                ba = fm_mm(wpa_d[n], yaT, "yaT")
                bm = fm_mm(wpm_d[n], hmT, "hmT")
                b0 = fm_mm(wfm_d[FM_G + n], hT, "hT")
                b1 = fm_mm(wfm_d[FM_G + 8 + n], hT, "hT")
                g0, g0k = tf()
                ACT(g0, ps(b0), AF.Sigmoid, [pk(b0), "cf"], [g0k], bias=bfm(FM_G + n))
                g1, g1k = tf()
                ACT(g1, ps(b1), AF.Sigmoid, [pk(b1), "cf"], [g1k], bias=bfm(FM_G + 8 + n))
                TT(g0, ps(ba), g0, ALU.mult, [pk(ba), g0k], [g0k])
                TT(g1, ps(bm), g1, ALU.mult, [pk(bm), g1k], [g1k])
                TT(mgT[:, n, :], g0, g1, ALU.add, [g0k, g1k], ["aqT"])
            for n in range(8):
                b = fm_mm(wo_d[n], mgT, "aqT")
                TT(X["t"][:, n, :], ps(b), X["t"][:, n, :], ALU.add, [pk(b), xk(n)], [xk(n)])
                if hook is not None:
                    if n >= 2:
                        hook(n - 2)
                    if n == 7:
                        hook(6)
                        hook(7)

        def dump(name, tile_ap, key):
            if name in dbg_d:
                P.dma("sp", dbg_d[name], tile_ap, reads=[key], semkey="dbg_" + name)

        tgs = list(range(4 - n_tg_prev, 4)) + list(range(4, 4 + n_tg_own))
        hoisted = {"n0": False}
        n_prev_run = n_tg_prev
        for pos, tg in enumerate(tgs):
            own = tg >= 4
            P.epoch = pos + 1
            t = pos
            X["t"], X["i"] = xbuf[pos % 2], pos % 2
            if pos == 0:
                P.dma("sp", xbuf[0][:], xin[:, :, tg * 512:(tg + 1) * 512].rearrange("k p n -> p k n"),
                      writes=xall(), semkey=("xld", 0))
            if pos + 1 < len(tgs):
                ntg, ni = tgs[pos + 1], (pos + 1) % 2
                P.dma("sp", xbuf[ni][:], xin[:, :, ntg * 512:(ntg + 1) * 512].rearrange("k p n -> p k n"),
                      writes=[("x", ni, kc) for kc in range(8)], semkey=("xld", ni))
            if not hoisted["n0"]:
                norm(0, hT, "hT")
            hoisted["n0"] = False
            na = NormAcc()
            ffn(1, hook=na.add)
            if level >= 2:
                na.finish(1, hT, "hT")
                w_in(t, own)
                if (not own) and pos + 1 < len(tgs):
                    X["t"], X["i"] = xbuf[(pos + 1) % 2], (pos + 1) % 2
                    norm(0, hT, "hT")
                    X["t"], X["i"] = xbuf[pos % 2], pos % 2
                    hoisted["n0"] = True
            if own and level >= 3:
                attention(t)
            if level >= 4:
                mlstm(t, own)
                halo_copy()
            if own and level < 5:
                P.dma("sp", out_d[:, :, (tg - 4) * 512:(tg - 3) * 512].rearrange("k p n -> p k n"), X["t"][:],
                      reads=xall() + ["aqT", "ak2T", "av2", "moT", "qkT", "mv", "yaT", "hmT", "C"], semkey="out")
            if own and level >= 5:
                if tg == 4:
                    dump("ya", yaT[:].rearrange("p k n -> p (k n)"), "yaT")
                    dump("hm", hmT[:].rearrange("p k n -> p (k n)"), "hmT")
                    dump("qk", qkT[:].rearrange("p k n -> p (k n)"), "qkT")
                na = NormAcc()
                proj(hook=na.add)
                if tg == 4:
                    pass
                na.finish(2, hT, "hT")
                na = NormAcc()
                ffn(2, hook=na.add)
                na.finish(3, None, "xT")
                P.dma("sp", out_d[:, :, (tg - 4) * 512:(tg - 3) * 512].rearrange("k p n -> p k n"), X["t"][:],
                      reads=xall(), semkey="out")
        fw = ["out"] + ["dbg_" + n for n in dbg_d]
        P.emit(final_wait_keys=fw)
    return nc


def _tile_fm(W, cols=None):
    Wc = W if cols is None else W[:, cols]
    n = Wc.shape[1] // 128
    return np.ascontiguousarray(Wc.reshape(8, 128, n, 128).transpose(2, 1, 0, 3)).reshape(n, 128, 1024)


def _fm_cols():
    cols = [np.arange(AQ, AQ + 1024)]
    z = np.full(64, ZC)
    for g in range(4):
        c = np.arange(AK + g * 64, AK + g * 64 + 64)
        cols.append(np.concatenate([c, z]))
        cols.append(np.concatenate([z, c]))
    cols.append(np.arange(MO, MO + 1024))
    for base in (MF, MI):
        for c in range(4):
            cols.append(np.concatenate([np.full(64, base + 2 * c), np.full(64, base + 2 * c + 1)]))
    cols.append(np.arange(MQ, MQ + 512))
    cols.append(np.arange(MK, MK + 512))
    cols.append(np.arange(GP, GP + 2048))
    return np.concatenate(cols)


def _tm_cols():
    cols = []
    for g in range(4):
        c = np.arange(AV + g * 64, AV + g * 64 + 64)
        cols.append(np.concatenate([c, c]))
    cols.append(np.arange(MV, MV + 1024))
    return np.concatenate(cols)


def _prep_common(inp):
    f = lambda a: np.ascontiguousarray(np.asarray(a, dtype=np.float32))
    w_in = np.concatenate([f(inp["w_in"])[0], np.zeros((1024, 1), np.float32)], axis=1)
    b_in = np.concatenate([f(inp["b_in"])[0], np.zeros(1, np.float32)])
    fmc, tmc = _fm_cols(), _tm_cols()
    com = {}
    for i, k in ((1, "ffn1"), (2, "ffn2")):
        com["wg%d" % i] = _tile_fm(f(inp[k + "_w_gate"])[0])
        com["wu%d" % i] = _tile_fm(f(inp[k + "_w_up"])[0])
        com["wd%d" % i] = np.ascontiguousarray(f(inp[k + "_w_down"])[0].reshape(NMF, 128, 1024))
    com["wfm"] = _tile_fm(w_in, fmc)
    wt = w_in[:, tmc]
    com["wtm"] = np.ascontiguousarray(wt.reshape(8, 128, 6, 256).transpose(2, 1, 0, 3)).reshape(6, 128, 2048)
    com["wpa"] = _tile_fm(f(inp["w_proj_attn"])[0])
    com["wpm"] = _tile_fm(f(inp["w_proj_mlstm"])[0])
    com["wo"] = _tile_fm(f(inp["w_out"])[0])
    cf = np.zeros((128, NCF), np.float32)
    cf[:, C_BFM:C_BFM + NFM] = b_in[fmc].reshape(NFM, 128).T
    gains = [f(inp["ffn1_norm"])[0], f(inp["mix_norm"])[0], f(inp["ffn2_norm"])[0], f(inp["final_norm"])]
    for gi, g in enumerate(gains):
        cf[:, C_GAIN + gi * 8:C_GAIN + gi * 8 + 8] = g.reshape(8, 128).T
    cf[:, C_HN:C_HN + 8] = f(inp["mlstm_head_norm"])[0].T
    conv = f(inp["mlstm_conv"])[0]
    cf[:, C_CONV:C_CONV + 32] = conv.reshape(4, 8, 128).transpose(2, 1, 0).reshape(128, 32)
    sinks = f(inp["attn_sinks"])[0]
    perm = [4 * g + 2 * cc + bi for g in range(4) for bi in range(2) for cc in range(2)]
    cf[:, C_SINK:C_SINK + 16] = sinks[perm][None, :]
    cf[:, C_BTM:C_BTM + 1536] = b_in[tmc][None, :]
    sm = np.ones(512, np.float32)
    sm[::128] = 0.0
    cf[:, C_SCAN:C_SCAN + 512] = sm[None, :]
    cf[:, C_EPS] = 1e-6
    cf[:, C_ONE] = 1.0
    cf[:, C_LN8] = np.float32(np.log(0.125))
    cf[:, C_TINY] = 1e-30
    k = np.arange(128)
    cb = np.zeros((128, 512), np.float32)
    cb[:, 0:128] = (k[:, None] <= k[None, :])
    cb[:, 128:256] = (k[:, None] > k[None, :])
    cb[:, 256:384] = np.eye(128, dtype=np.float32)
    cb[:, 384:512] = 1.0
    com["cb"] = cb
    return com, cf


def _core_inputs(x, com, cf, core):
    b, hf = core // 2, core % 2
    xin = np.zeros((1024, 4096), np.float32)
    if hf == 1:
        xin[:, 0:2048] = x[b, 0:2048].T
    xin[:, 2048:4096] = x[b, hf * 2048:(hf + 1) * 2048].T
    cfc = cf.copy()
    cfc[:, C_FLAG] = float(hf)
    d = dict(com)
    d["xT_in"] = np.ascontiguousarray(xin.reshape(8, 128, 4096))
    d["cf"] = cfc
    return d


def kernel(**inputs):
    x = np.asarray(inputs["x"], dtype=np.float32)
    com, cf = _prep_common(inputs)
    nc = build_program()
    in_maps = [_core_inputs(x, com, cf, c) for c in range(8)]
    res = run_bass_kernel_spmd(nc, in_maps, core_ids=list(range(8)))
    out = np.empty((4, 4096, 1024), np.float32)
    for c in range(8):
        o = np.asarray(res.results[c]["outT"]).reshape(1024, 2048)
        out[c // 2, (c % 2) * 2048:(c % 2 + 1) * 2048, :] = o.T
    return out
```

```python
import contextlib
import numpy as np
import concourse.bass as bass
import concourse.mybir as mybir
from concourse.bass_utils import run_bass_kernel_spmd

F32 = mybir.dt.float32
BF16 = mybir.dt.bfloat16
AF = mybir.ActivationFunctionType
ALU = mybir.AluOpType
ENGS = ("pe", "act", "dve", "pool", "sp")

D_MODEL = 1024
D_FF = 2816
NMF = D_FF // 128
AQ, AK, AV, MQ, MK, MV, MO, MI, MF, GP = 0, 1024, 1280, 1536, 2048, 2560, 3584, 4608, 4616, 4624
NFM = 56
FM_AQ, FM_AK, FM_MO, FM_F, FM_I, FM_MQ, FM_MK, FM_G = 0, 8, 16, 24, 28, 32, 36, 40
ZC = 6672
C_BFM, C_GAIN, C_HN, C_CONV, C_SINK, C_FLAG, C_BTM, C_SCAN, C_EPS, C_ONE, C_LN8, C_TINY, C_BG, C_SEL = (
    0, 56, 88, 96, 128, 144, 145, 1681, 2193, 2194, 2195, 2196, 2197, 2199)
NCF = 2199 + 512
import os
ATT_SUB = int(os.environ.get('ATT_SUB', '9'))


class Op:
    __slots__ = ("eng", "fn", "reads", "writes", "dma", "deps", "signal", "cnt", "semkey", "idx", "epoch", "eseq")


class Prog:
    def __init__(self, nc):
        self.nc = nc
        self.ops = []
        self.last_w = {}
        self.readers = {}
        self.epoch = 0
        self.eseq = {e: 0 for e in ENGS}

    def op(self, eng, fn, reads=(), writes=(), dma=False, semkey=None):
        o = Op()
        o.eng, o.fn, o.reads, o.writes, o.dma, o.semkey = eng, fn, tuple(reads), tuple(writes), dma, semkey
        o.signal, o.cnt, o.epoch = False, 0, self.epoch
        o.idx = len(self.ops)
        o.eseq = self.eseq[eng]
        self.eseq[eng] += 1
        raw = set()
        deps = set()
        for k in o.reads:
            w = self.last_w.get(k)
            if w is not None:
                deps.add(w)
                raw.add(w)
        for k in o.writes:
            w = self.last_w.get(k)
            if w is not None:
                deps.add(w)
            deps.update(self.readers.get(k, {}).values())
        need = []
        for d in deps:
            dop = self.ops[d]
            if dop.dma or dop.eng != eng:
                need.append(d)
            elif eng != "pe" and not dma:
                if d in raw or (o.eseq - dop.eseq) <= 3:
                    need.append(d)
        o.deps = need
        for d in need:
            self.ops[d].signal = True
        for k in o.reads:
            rd = self.readers.setdefault(k, {})
            rd[("dma", o.idx) if dma else eng] = o.idx
        for k in o.writes:
            self.last_w[k] = o.idx
            self.readers[k] = {}
        self.ops.append(o)
        return o

    def dma(self, q, out, in_, reads=(), writes=(), semkey=None):
        return self.op(q, lambda e: e.dma_start(out=out, in_=in_), reads, writes, dma=True, semkey=semkey)

    def emit(self, final_wait_keys=()):
        nc = self.nc
        eng_cnt = {}
        dma_cnt = {}
        for o in self.ops:
            if o.dma:
                dma_cnt[o.semkey] = dma_cnt.get(o.semkey, 0) + 16
                o.cnt = dma_cnt[o.semkey]
            elif o.signal:
                ek = (o.eng, o.epoch)
                eng_cnt[ek] = eng_cnt.get(ek, 0) + 1
                o.cnt = eng_cnt[ek]
        semkeys = sorted(dma_cnt.keys(), key=str)
        with contextlib.ExitStack() as st:
            esem = {ek: st.enter_context(nc.semaphore("s_%s_%d" % ek)) for ek in sorted(eng_cnt.keys())}
            dsem = {k: st.enter_context(nc.semaphore("d_%d" % i)) for i, k in enumerate(semkeys)}
            block = st.enter_context(nc.Block())
            ops = self.ops

            def run(engname, eng):
                waited = {}
                for o in ops:
                    if o.eng != engname:
                        continue
                    for d in o.deps:
                        dop = ops[d]
                        if dop.dma:
                            sem, val, key = dsem[dop.semkey], dop.cnt, ("d", dop.semkey)
                        else:
                            ek = (dop.eng, dop.epoch)
                            sem, val, key = esem[ek], dop.cnt, ("e", ek)
                        if waited.get(key, 0) >= val:
                            continue
                        waited[key] = val
                        eng.wait_ge(sem, val)
                    ins = o.fn(eng)
                    if o.dma:
                        ins.then_inc(dsem[o.semkey], 16)
                    elif o.signal:
                        ins.then_inc(esem[(o.eng, o.epoch)], 1)
                if engname == "sp":
                    for k in final_wait_keys:
                        eng.wait_ge(dsem[k], dma_cnt[k])

            block.tensor(lambda e: run("pe", e))
            block.scalar(lambda e: run("act", e))
            block.vector(lambda e: run("dve", e))
            block.gpsimd(lambda e: run("pool", e))
            block.sync(lambda e: run("sp", e))


def build_program(n_tg_prev=4, n_tg_own=4, dbg=None, level=9):
    nc = bass.Bass("TRN2", target_bir_lowering=False)

    def DI(name, shape):
        return nc.dram_tensor(name, shape, F32, kind="ExternalInput").ap()

    xin = DI("xT_in", [8, 128, 4096])
    cf_d = DI("cf", [128, NCF])
    cb_d = DI("cb", [128, 512])
    wffn = {1: (DI("wg1", [NMF, 128, 1024]), DI("wu1", [NMF, 128, 1024]), DI("wd1", [NMF, 128, 1024])),
            2: (DI("wg2", [NMF, 128, 1024]), DI("wu2", [NMF, 128, 1024]), DI("wd2", [NMF, 128, 1024]))}
    wfm_d = DI("wfm", [NFM, 128, 1024])
    wtm_d = DI("wtm", [6, 128, 2048])
    wgt_d = DI("wgt", [128, 128])
    wpa_d = DI("wpa", [8, 128, 1024])
    wpm_d = DI("wpm", [8, 128, 1024])
    wo_d = DI("wo", [8, 128, 1024])
    out_d = nc.dram_tensor("outT", [8, 128, 2048], F32, kind="ExternalOutput").ap()
    dbg_d = {}
    if dbg:
        for name, shape in dbg.items():
            dbg_d[name] = nc.dram_tensor("dbg_" + name, shape, F32, kind="ExternalOutput").ap()

    with contextlib.ExitStack() as st:
        def SB(name, shape, dt):
            return st.enter_context(nc.sbuf_tensor("sb_" + name, shape, dt))

        P = Prog(nc)
        xbuf = [SB("xT%d" % i, [128, 8, 512], F32) for i in range(2)]
        X = {"t": xbuf[0], "i": 0}

        def xk(kc):
            return ("x", X["i"], kc)

        def xall():
            return [("x", X["i"], kc) for kc in range(8)]
        hT = SB("hT", [128, 8, 512], BF16)
        MG = 4
        aT = SB("aT", [128, MG, 512], BF16)
        cf = SB("cfs", [128, NCF], F32)
        cb = SB("cbs", [128, 512], BF16)
        aqT = SB("aqT", [128, 8, 512], BF16)
        ak2T = SB("ak2T", [128, 8, 640], BF16)
        kz = SB("kz", [128, 8, 512], BF16)
        av2 = SB("av2", [128, 5, 512], BF16)
        moT = SB("moT", [128, 8, 512], BF16)
        qkT = SB("qkT", [128, 4, 512], BF16)
        mv = SB("mv", [128, 4, 8, 130], BF16)
        yaT = SB("yaT", [128, 8, 512], BF16)
        hmT = SB("hmT", [128, 8, 512], BF16)
        mgT = aqT
        Cst = SB("Cst", [128, 4, 129], F32)
        Cb = SB("Cb", [128, 2, 8, 128], BF16)
        ksum = SB("ksum", [128, 2, 8], F32)
        Nb = SB("Nb", [128, 2, 8, 128], BF16)
        eg = SB("eg", [128, 4, 4], F32)
        halo = SB("halo", [128, 8, 3], F32)
        b15 = SB("b15", [128, 8], F32)
        geb = SB("geb", [8, 512], F32)
        gek = SB("gek", [8, 512], F32)
        esink = SB("esink", [128, 16], F32)
        mpf = SB("mpf", [128, 128], BF16)
        mbp = SB("mbp", [128, 512], BF16)
        mbc = SB("mbc", [128, 512], BF16)
        mbf = SB("mbf", [128, 512], BF16)
        esrow = SB("esrow", [1, 2048], BF16)
        pre = [SB("pre%d" % i, [128, 515], F32) for i in range(2)]
        rings = {
            "w": [SB("w%d" % i, [128, 1024], BF16) for i in range(8)],
            "wd": [SB("wd%d" % i, [128, 1024], BF16) for i in range(2 * MG)],
            "tf": [SB("tf%d" % i, [128, 512], F32) for i in range(8)],
            "tb": [SB("tb%d" % i, [128, 512], BF16) for i in range(8)],
        }
        rpos = {k: 0 for k in rings}
        psum = st.enter_context(nc.psum_tensor("psum", [128, 4096], F32))
        pstate = {"i": 0}

        live_banks = set()

        def nb():
            for _ in range(7):
                b = pstate["i"] % 7
                pstate["i"] += 1
                if b not in live_banks:
                    live_banks.add(b)
                    return b
            raise AssertionError("all PSUM banks live")

        _orig_op = P.op

        def _op(eng, fn, reads=(), writes=(), dma=False, semkey=None):
            for k in reads:
                if isinstance(k, tuple) and k[0] == "ps":
                    live_banks.discard(k[1])
            return _orig_op(eng, fn, reads, writes, dma, semkey)

        P.op = _op

        def ps(b):
            return psum[:, b * 512:(b + 1) * 512]

        def pk(b):
            return ("ps", b)

        def rnext(name):
            i = rpos[name] % len(rings[name])
            rpos[name] += 1
            return rings[name][i], (name, i)

        def tf():
            t, k = rnext("tf")
            return t[:], k

        def tb():
            t, k = rnext("tb")
            return t[:], k

        def ring_load(name, src):
            t, k = rnext(name)
            P.dma("pool", t[:], src, writes=[k], semkey=k)
            return t, k

        def MM(out, lhsT, rhs, start, stop, reads, writes):
            P.op("pe", lambda e: e.matmul(out, lhsT=lhsT, rhs=rhs, start=start, stop=stop), reads, writes)

        def ACT(out, in_, func, reads, writes, bias=None, scale=1.0):
            if bias is None:
                P.op("act", lambda e: e.activation(out=out, in_=in_, func=func, scale=scale), reads, writes)
            else:
                P.op("act", lambda e: e.activation(out=out, in_=in_, func=func, bias=bias, scale=scale), reads, writes)

        def TT(out, in0, in1, op, reads, writes, eng="dve"):
            P.op(eng, lambda e: e.tensor_tensor(out=out, in0=in0, in1=in1, op=op), reads, writes)

        def TS(out, in0, s1, op0, reads, writes, eng="dve"):
            P.op(eng, lambda e: e.tensor_scalar(out=out, in0=in0, scalar1=s1, scalar2=None, op0=op0), reads, writes)

        def STT(out, in0, scalar, in1, op0, op1, reads, writes, eng="dve"):
            P.op(eng, lambda e: e.scalar_tensor_tensor(out=out, in0=in0, scalar=scalar, in1=in1, op0=op0, op1=op1),
                 reads, writes)

        def TC(out, in_, reads, writes, eng="dve"):
            P.op(eng, lambda e: e.tensor_copy(out=out, in_=in_), reads, writes)

        def RCP(out, in_, reads, writes):
            P.op("dve", lambda e: e.reciprocal(out=out, in_=in_), reads, writes)

        def r4(ap):
            return ap.rearrange("p (a b) -> p a b", a=4)

        def r2(ap):
            return ap.rearrange("p (a b) -> p a b", a=2)

        P.dma("sp", cf[:], cf_d, writes=["cf"], semkey="cf")
        P.dma("pool", cb[:], cb_d, writes=["cb"], semkey="cb")
        mask_cur, mask_prev, ident, ones = cb[:, 0:128], cb[:, 128:256], cb[:, 256:384], cb[:, 384:512]
        flag = cf[:, C_FLAG:C_FLAG + 1]
        eps = cf[:, C_EPS:C_EPS + 1]
        one = cf[:, C_ONE:C_ONE + 1]
        ln8 = cf[:, C_LN8:C_LN8 + 1]
        tiny = cf[:, C_TINY:C_TINY + 1]
        scanmask = cf[:, C_SCAN:C_SCAN + 512]
        btm = cf[:, C_BTM:C_BTM + 1536]

        def bfm(ci):
            return cf[:, C_BFM + ci:C_BFM + ci + 1]

        TS(b15[0:8, 0:2], cf[0:8, C_BG:C_BG + 2], 1.0 / 15.0, ALU.mult, ["cf"], ["b15"])
        ACT(esink[:], cf[:, C_SINK:C_SINK + 16], AF.Exp, ["cf"], ["esink"])
        TS(mpf[:], mask_prev, flag, ALU.mult, ["cb", "cf"], ["mpf"])
        for (dst, src, k_) in ((mbp, mask_prev, "cb"), (mbc, mask_cur, "cb"), (mbf, mpf[:], "mpf")):
            P.op("dve", lambda e, dst=dst, src=src: e.tensor_scalar(
                out=r4(dst[:]), in0=src.unsqueeze(1).to_broadcast([128, 4, 128]), scalar1=-1.0, scalar2=30000.0,
                op0=ALU.add, op1=ALU.mult), [k_], ["mbp" if dst is not mbf else "mbf"])
        TC(esrow[0:1, :].rearrange("p (a b) -> p a b", a=16), esink[0:1, :].unsqueeze(2).to_broadcast([1, 16, 128]),
           ["esink"], ["esrow"])
        P.op("dve", lambda e: e.memset(halo[:], 0.0), writes=["halo"])
        P.op("dve", lambda e: e.memset(Cst[:], 0.0), writes=["C"])
        P.op("dve", lambda e: e.memset(Cb[:], 0.0), writes=[("Cb", 0), ("Cb", 1)])
        P.op("dve", lambda e: e.memset(Nb[:], 0.0), writes=[("Cb", 0), ("Cb", 1)])
        P.op("dve", lambda e: e.memset(mv[:], 1.0), writes=["mv"])
        P.op("dve", lambda e: e.memset(ak2T[:], 0.0), writes=["ak2T"])
        P.op("dve", lambda e: e.memset(kz[:], 0.0), writes=["kz"])
        P.op("dve", lambda e: e.memset(av2[:], 0.0), writes=["av2"])

        class NormAcc:
            def __init__(self):
                self.b = 7
                self.n = 0

            def add(self, kc):
                sq, sqk = tb()
                ACT(sq, X["t"][:, kc, :], AF.Square, [xk(kc)], [sqk])
                MM(ps(self.b), ones, sq, self.n == 0, self.n == 7, [sqk, "cb"], [pk(self.b)])
                self.n += 1

            def finish(self, gi, dst, dkey):
                assert self.n == 8
                b = self.b
                rs, rsk = tf()
                ACT(rs, ps(b), AF.Ln, [pk(b), "cf"], [rsk], bias=eps, scale=1.0 / D_MODEL)
                ACT(rs, rs, AF.Exp, [rsk], [rsk], scale=-0.5)
                for kc in range(8):
                    g = cf[:, C_GAIN + gi * 8 + kc:C_GAIN + gi * 8 + kc + 1]
                    dk = (dkey, kc) if dkey == "hT" else xk(kc)
                    d_ = dst if dkey == "hT" else X["t"]
                    STT(d_[:, kc, :], X["t"][:, kc, :], g, rs, ALU.mult, ALU.mult, [xk(kc), rsk, "cf"], [dk])

        def norm(gi, dst, dkey):
            na = NormAcc()
            for kc in range(8):
                na.add(kc)
            na.finish(gi, dst, dkey)

        def ffn(which, hook=None):
            wg_d, wu_d, wd_d = wffn[which]
            m0 = 0
            while m0 < NMF:
                grp = list(range(m0, min(m0 + MG, NMF)))
                m0 += MG
                wds = []
                for j, m in enumerate(grp):
                    wg_t, wg_k = ring_load("w", wg_d[m])
                    wu_t, wu_k = ring_load("w", wu_d[m])
                    wds.append(ring_load("wd", wd_d[m]))
                    bg, bu = nb(), nb()
                    for kc in range(8):
                        MM(ps(bg), wg_t[:, kc * 128:(kc + 1) * 128], hT[:, kc, :], kc == 0, kc == 7,
                           [wg_k, ("hT", kc)], [pk(bg)])
                    for kc in range(8):
                        MM(ps(bu), wu_t[:, kc * 128:(kc + 1) * 128], hT[:, kc, :], kc == 0, kc == 7,
                           [wu_k, ("hT", kc)], [pk(bu)])
                    s, sk = tf()
                    ACT(s, ps(bg), AF.Silu, [pk(bg)], [sk])
                    TT(aT[:, j, :], s, ps(bu), ALU.mult, [sk, pk(bu)], [("aT", j)])
                for n in range(8):
                    by = nb()
                    for j in range(len(grp)):
                        wd_t, wd_k = wds[j]
                        MM(ps(by), wd_t[:, n * 128:(n + 1) * 128], aT[:, j, :], j == 0, j == len(grp) - 1,
                           [wd_k, ("aT", j)], [pk(by)])
                    STT(X["t"][:, n, :], ps(by), 0.5, X["t"][:, n, :], ALU.mult, ALU.add, [pk(by), xk(n)], [xk(n)])
                    if hook is not None and m0 >= NMF:
                        if n >= 2:
                            hook(n - 2)
                        if n == 7:
                            hook(6)
                            hook(7)

        def fm_mm(src, rhs_t, rhs_key):
            wt, wk = ring_load("w", src)
            b = nb()
            for kc in range(8):
                rk_ = (rhs_key, kc) if rhs_key == "hT" else rhs_key
                MM(ps(b), wt[:, kc * 128:(kc + 1) * 128], rhs_t[:, kc, :], kc == 0, kc == 7, [wk, rk_], [pk(b)])
            return b

        def w_in(t, own):
            first_own = (t == n_tg_prev)
            need_halo = own or (t == n_tg_prev - 1)
            tm_todo = [gi for gi in range(6) if (gi >= 2 or need_halo)]

            def tm_group():
                if not tm_todo:
                    return
                gi = tm_todo.pop(0)
                halves = [ring_load("w", wtm_d[gi][:, hf_ * 1024:(hf_ + 1) * 1024]) for hf_ in range(2)]
                for blk in range(4):
                    b = nb()
                    for kc in range(8):
                        wt, wk = halves[kc // 4]
                        MM(ps(b)[:, 0:256], hT[:, kc, blk * 128:(blk + 1) * 128], wt[:, (kc % 4) * 256:(kc % 4 + 1) * 256],
                           kc == 0, kc == 7, [wk, ("hT", kc)], [pk(b)])
                    bias = btm[:, gi * 256:(gi + 1) * 256]
                    if gi < 2:
                        TT(av2[:, 1 + blk, gi * 256:(gi + 1) * 256], ps(b)[:, 0:256], bias, ALU.add, [pk(b), "cf"], ["av2"])
                    else:
                        h0 = (gi - 2) * 2
                        TT(mv[:, blk, h0:h0 + 2, 0:128], r2(ps(b)[:, 0:256]), r2(bias), ALU.add, [pk(b), "cf"], ["mv"])

            if own:
                for c in range(8):
                    b = fm_mm(wfm_d[c], hT, "hT")
                    ACT(aqT[:, c, :], ps(b), AF.Identity, [pk(b), "cf"], ["aqT"], bias=bfm(c))
                    if c == 3:
                        tm_group()
            if need_halo:
                tm_group()
            for g in range(4):
                if not need_halo:
                    break
                b = fm_mm(wfm_d[FM_AK + g], hT, "hT")
                ACT(ak2T[0:64, 2 * g, 128:640], ps(b)[0:64, :], AF.Identity, [pk(b), "cf"], ["ak2T"],
                    bias=cf[0:64, C_BFM + FM_AK + g:C_BFM + FM_AK + g + 1])
                ACT(ak2T[64:128, 2 * g + 1, 128:640], ps(b)[64:128, :], AF.Identity, [pk(b), "cf"], ["ak2T"],
                    bias=cf[64:128, C_BFM + FM_AK + g:C_BFM + FM_AK + g + 1])
            if own:
                for h in range(8):
                    b = fm_mm(wfm_d[FM_MO + h], hT, "hT")
                    s, sk = tf()
                    ACT(s, ps(b), AF.Sigmoid, [pk(b), "cf"], [sk], bias=bfm(FM_MO + h))
                    TS(moT[:, h, :], s, cf[:, C_HN + h:C_HN + h + 1], ALU.mult, [sk, "cf"], ["moT"])
            need_q = own or (t == n_tg_prev - 1)
            wg_t, wg_k = rnext("w")
            P.dma("pool", wg_t[:, 0:128], wgt_d, writes=[wg_k], semkey=wg_k)
            b_f, b_i = nb(), nb()
            for kc in range(8):
                MM(ps(b_f)[0:8, :], wg_t[:, kc * 16:kc * 16 + 8], hT[:, kc, :], kc == 0, kc == 7,
                   [wg_k, ("hT", kc)], [pk(b_f)])
            for kc in range(8):
                MM(ps(b_i)[0:8, :], wg_t[:, kc * 16 + 8:kc * 16 + 16], hT[:, kc, :], kc == 0, kc == 7,
                   [wg_k, ("hT", kc)], [pk(b_i)])
            t1_, t1k = tf()
            t4_, t4k = tf()
            bt_, btk = tf()
            t1, t4, bt = t1_[0:8, :], t4_[0:8, :], bt_[0:8, :]
            ACT(t1, ps(b_f)[0:8, :], AF.Tanh, [pk(b_f), "b15"], [t1k], bias=b15[0:8, 0:1], scale=1.0 / 15.0)
            ACT(t4, ps(b_i)[0:8, :], AF.Tanh, [pk(b_i), "b15"], [t4k], bias=b15[0:8, 1:2], scale=1.0 / 15.0)
            ACT(t1, t1, AF.Exp, [t1k], [t1k], scale=-15.0)
            ACT(t1, t1, AF.Ln, [t1k, "cf"], [t1k], bias=one[0:8, :])
            P.op("dve", lambda e: e.tensor_tensor_scan(
                out=bt, data0=scanmask[0:8, :], data1=t1, initial=0.0, op0=ALU.mult, op1=ALU.subtract),
                [t1k, "cf"], [btk])
            ACT(geb[0:8, :], bt, AF.Exp, [btk], ["geb"])
            STT(t4, t4, 15.0, bt, ALU.mult, ALU.subtract, [t4k, btk], [t4k])
            ACT(gek[0:8, :], t4, AF.Exp, [t4k, "cf"], ["gek"], bias=ln8[0:8, :])
            for c in range(4):
                tm_group_pair = None
                chunks = []
                for (ci, qc) in ((FM_MQ + c, c), (FM_MK + c, 4 + c)):
                    if qc < 4 and not need_q:
                        continue
                    chunks.append((ci, qc, fm_mm(wfm_d[ci], hT, "hT")))
                pres = []
                for k_, (ci, qc, b) in enumerate(chunks):
                    pi = rpos.setdefault("pre", 0) % 2
                    rpos["pre"] += 1
                    pt, pkk = pre[pi], ("pre", pi)
                    ACT(pt[:, 3:515], ps(b), AF.Identity, [pk(b), "cf"], [pkk], bias=bfm(ci))
                    pres.append((pt, pkk))
                accs = []
                for k_, (ci, qc, b) in enumerate(chunks):
                    pt, pkk = pres[k_]
                    if first_own:
                        TS(pt[:, 0:3], halo[:, qc, :], flag, ALU.mult, ["halo", "cf"], [pkk])
                    else:
                        TC(pt[:, 0:3], halo[:, qc, :], ["halo"], [pkk])
                    acc, ack = tf()
                    cw = C_CONV + qc * 4
                    TS(acc, pt[:, 0:512], cf[:, cw:cw + 1], ALU.mult, [pkk, "cf"], [ack])
                    for j in range(1, 4):
                        STT(acc, pt[:, j:j + 512], cf[:, cw + j:cw + j + 1], acc, ALU.mult, ALU.add,
                            [pkk, "cf", ack], [ack])
                    TC(halo[:, qc, :], pt[:, 512:515], [pkk], ["halo"])
                    accs.append((acc, ack, qc))
                for (acc, ack, qc) in accs:
                    ACT(acc, acc, AF.Silu, [ack], [ack])
                selc = cf[0:8, C_SEL + c * 128:C_SEL + (c + 1) * 128]
                b_eb = nb()
                MM(ps(b_eb), selc, geb[0:8, :], True, True, ["cf", "geb"], [pk(b_eb)])
                b_ek = nb()
                MM(ps(b_ek), selc, gek[0:8, :], True, True, ["cf", "gek"], [pk(b_ek)])
                TC(eg[:, c, :], r4(ps(b_eb))[:, :, 127], [pk(b_eb)], ["eg"])
                for (acc, ack, qc) in accs:
                    if qc < 4:
                        TT(qkT[:, qc, :], acc, ps(b_eb), ALU.mult, [ack, pk(b_eb)], ["qkT"])
                for (acc, ack, qc) in accs:
                    if qc >= 4:
                        TT(kz[0:64, 2 * c, :], acc[0:64, :], ps(b_ek)[0:64, :], ALU.mult, [ack, pk(b_ek)], ["kz"])
                        TT(kz[64:128, 2 * c + 1, :], acc[64:128, :], ps(b_ek)[64:128, :], ALU.mult, [ack, pk(b_ek)], ["kz"])
            while tm_todo:
                tm_group()
        def attention(t):
            first_own = (t == n_tg_prev)
            items = [(blk, g) for blk in range(4) for g in range(4)]
            Ems = {}

            def stage_a(blk, g):
                qs = slice(blk * 128, (blk + 1) * 128)
                Em = []
                for kb in range(2):
                    b = nb()
                    ks = slice((blk + kb) * 128, (blk + kb + 1) * 128)
                    if kb == 0:
                        mb_, mk_ = (mbf[:], "mbf") if (first_own and blk == 0) else (mbp[:], "mbp")
                    else:
                        mb_, mk_ = mbc[:], "mbp"
                    MM(ps(b), ident, mb_, True, False, ["cb", mk_], [pk(b)])
                    for bi in range(2):
                        MM(r2(ps(b)[:, bi * 256:(bi + 1) * 256]), ak2T[:, 2 * g + bi, ks], aqT[:, 2 * g:2 * g + 2, qs],
                           False, bi == 1, ["ak2T", "aqT"], [pk(b)])
                    E, Ek = tb()
                    ACT(E, ps(b), AF.Exp, [pk(b)], [Ek], scale=0.125)
                    Em.append((E, Ek))
                Ems[(blk, g)] = Em

            def stage_b(blk, g):
                qs = slice(blk * 128, (blk + 1) * 128)
                Em = Ems.pop((blk, g))
                bo, bd = nb(), nb()
                for kb in range(2):
                    MM(ps(bo), av2[:, blk + kb, g * 128:(g + 1) * 128], Em[kb][0], kb == 0, kb == 1,
                       ["av2", Em[kb][1]], [pk(bo)])
                for kb in range(2):
                    MM(ps(bd), ones, Em[kb][0], kb == 0, False, ["cb", Em[kb][1]], [pk(bd)])
                MM(ps(bd), ones[0:1, :], esrow[0:1, g * 512:(g + 1) * 512], False, True, ["cb", "esrow"], [pk(bd)])
                r, rk = tf()
                ACT(r, ps(bd), AF.Ln, [pk(bd)], [rk])
                ACT(r, r, AF.Exp, [rk], [rk], scale=-1.0)
                for bi in range(2):
                    rs_ = slice(bi * 64, bi * 64 + 64)
                    cs = slice(bi * 256, (bi + 1) * 256)
                    TT(yaT[rs_, 2 * g:2 * g + 2, qs], r2(ps(bo)[rs_, cs]), r2(r[rs_, cs]), ALU.mult,
                       [pk(bo), rk], ["yaT"])

            stage_a(*items[0])
            for i, it in enumerate(items):
                if i + 1 < len(items):
                    stage_a(*items[i + 1])
                stage_b(*it)

        def halo_copy():
            TC(ak2T[:, :, 0:128], ak2T[:, :, 512:640], ["ak2T"], ["ak2T"])
            TC(av2[:, 0, :], av2[:, 4, :], ["av2"], ["av2"])

        CbV = [[Cb[hh_ * 64:(hh_ + 1) * 64, v].rearrange("p (c h) w -> p c h w", h=2)[:, :, hh_, :] for hh_ in range(2)]
               for v in range(2)]
        NbV = [[Nb[hh_ * 64:(hh_ + 1) * 64, v].rearrange("p (c h) w -> p c h w", h=2)[:, :, hh_, :] for hh_ in range(2)]
               for v in range(2)]
        mst = {"g": 0, "ks": 0}

        def recast_state(v):
            for hh_ in range(2):
                hs = slice(hh_ * 64, hh_ * 64 + 64)
                TC(CbV[v][hh_], Cst[hs, :, 0:128], ["C"], [("Cb", v)])
                TC(NbV[v][hh_], Cst[hs, :, 128:129].to_broadcast([64, 4, 128]), ["C"], [("Cb", v)])

        def mlstm(t, own):
            if t == n_tg_prev:
                TS(Cst[:], Cst[:], flag, ALU.mult, ["C", "cf"], ["C"])
                recast_state(mst["g"] % 2)
            g0 = mst["g"]
            mst["g"] += 4
            Ub, Sm, post = {}, {}, {}

            def tu_mm(blk):
                qs = slice(blk * 128, (blk + 1) * 128)
                bK = nb()
                for c in range(4):
                    for hh_ in range(2):
                        MM(ps(bK)[:, c * 128:(c + 1) * 128], kz[:, 2 * c + hh_, qs], ident, hh_ == 0, hh_ == 1,
                           ["kz", "cb"], [pk(bK)])
                kt, ktk = tb()
                ACT(kt, ps(bK), AF.Identity, [pk(bK)], [ktk])
                bU = [nb(), nb()]
                for c in range(4):
                    MM(r2(ps(bU[c // 2])[:, (c % 2) * 256:(c % 2) * 256 + 256]), kt[:, c * 128:(c + 1) * 128],
                       mv[:, blk, 2 * c:2 * c + 2, 0:128], True, True, [ktk, "mv"], [pk(bU[c // 2])])
                Ub[blk] = bU

            def upd(blk):
                qs = slice(blk * 128, (blk + 1) * 128)
                bU = Ub.pop(blk)
                ki = mst["ks"] % 2
                mst["ks"] += 1
                P.op("dve", lambda e: e.reduce_sum(out=ksum[:, ki, :], in_=kz[:, :, qs], axis=mybir.AxisListType.X),
                     ["kz"], [("ks", ki)])
                for b_ in range(2):
                    for hh_ in range(2):
                        hs = slice(hh_ * 64, hh_ * 64 + 64)
                        uv = ps(bU[b_])[hs, :].rearrange("p (c h w) -> p c h w", c=2, h=2)[:, :, hh_, :]
                        TT(Cst[hs, 2 * b_:2 * b_ + 2, 0:128], uv, Cst[hs, 2 * b_:2 * b_ + 2, 0:128], ALU.add,
                           [pk(bU[b_]), "C"], ["C"])
                ks2 = ksum[:, ki, :].rearrange("p (c h) -> p c h", h=2)
                TT(Cst[:, :, 128], Cst[:, :, 128], ks2[:, :, 0], ALU.add, ["C", ("ks", ki)], ["C"])
                TT(Cst[:, :, 128], Cst[:, :, 128], ks2[:, :, 1], ALU.add, ["C", ("ks", ki)], ["C"])
                TT(Cst[:], Cst[:], eg[:, :, blk:blk + 1].to_broadcast([128, 4, 129]), ALU.mult, ["C", "eg"], ["C"])
                recast_state((g0 + blk + 1) % 2)

            def s_stage(blk):
                qs = slice(blk * 128, (blk + 1) * 128)
                bS = [nb(), nb()]
                for h in range(8):
                    c = h // 2
                    cs = slice((h % 4) * 128, (h % 4) * 128 + 128)
                    MM(ps(bS[h // 4])[:, cs], kz[:, h, qs], qkT[:, c, qs], True, True, ["qkT", "kz"], [pk(bS[h // 4])])
                sm = []
                for i in range(2):
                    s_, sk_ = tb()
                    TT(r4(s_), r4(ps(bS[i])), mask_cur.unsqueeze(1).to_broadcast([128, 4, 128]), ALU.mult,
                       [pk(bS[i]), "cb"], [sk_])
                    sm.append((s_, sk_))
                Sm[blk] = sm

            def nd_stage(blk):
                qs = slice(blk * 128, (blk + 1) * 128)
                v = (g0 + blk) % 2
                sm = Sm.pop(blk)
                bN = [nb(), nb()]
                bD = [nb(), nb()]
                for h in range(8):
                    c = h // 2
                    cs = slice((h % 4) * 128, (h % 4) * 128 + 128)
                    i = h // 4
                    MM(ps(bN[i])[:, cs], mv[:, blk, h, 0:128], sm[i][0][:, cs], True, False, ["mv", sm[i][1]], [pk(bN[i])])
                    MM(ps(bN[i])[:, cs], Cb[:, v, h, :], qkT[:, c, qs], False, True, [("Cb", v), "qkT"], [pk(bN[i])])
                    MM(ps(bD[i])[:, cs], ones, sm[i][0][:, cs], True, False, ["cb", sm[i][1]], [pk(bD[i])])
                    MM(ps(bD[i])[:, cs], Nb[:, v, h, :], qkT[:, c, qs], False, True, [("Cb", v), "qkT"], [pk(bD[i])])
                pp = []
                for i in range(2):
                    a, ak = tf()
                    ACT(a, ps(bD[i]), AF.Abs, [pk(bD[i])], [ak])
                    ACT(a, a, AF.Ln, [ak, "cf"], [ak], bias=tiny)
                    ACT(a, a, AF.Exp, [ak], [ak], scale=-1.0)
                    hh, hk = tf()
                    STT(hh, a, 1.0, ps(bN[i]), ALU.min, ALU.mult, [ak, pk(bN[i])], [hk])
                    sq, sqk = tb()
                    ACT(sq, hh, AF.Square, [hk], [sqk])
                    pp.append((hh, hk, sq, sqk))
                post[blk] = pp

            def ssq_stage(blk):
                qs = slice(blk * 128, (blk + 1) * 128)
                for i, (hh, hk, sq, sqk) in enumerate(post.pop(blk)):
                    bq = nb()
                    MM(ps(bq), ones, sq, True, True, ["cb", sqk], [pk(bq)])
                    rs, rsk = tf()
                    ACT(rs, ps(bq), AF.Ln, [pk(bq), "cf"], [rsk], bias=eps, scale=1.0 / 128.0)
                    ACT(rs, rs, AF.Exp, [rsk], [rsk], scale=-0.5)
                    TT(hh, hh, rs, ALU.mult, [hk, rsk], [hk])
                    TT(hmT[:, 4 * i:4 * i + 4, qs], r4(hh), moT[:, 4 * i:4 * i + 4, qs], ALU.mult,
                       [hk, "moT"], ["hmT"])

            if not own:
                for blk in range(4):
                    tu_mm(blk)
                    upd(blk)
                return
            tu_mm(0); upd(0); s_stage(0)
            tu_mm(1); s_stage(1)
            nd_stage(0); upd(1)
            tu_mm(2); s_stage(2)
            ssq_stage(0)
            nd_stage(1); upd(2)
            tu_mm(3); s_stage(3)
            ssq_stage(1)
            nd_stage(2); upd(3)
            ssq_stage(2)
            nd_stage(3)
            ssq_stage(3)

        def proj(hook=None):
            for n in range(8):
                ba = fm_mm(wpa_d[n], yaT, "yaT")
                bm = fm_mm(wpm_d[n], hmT, "hmT")
                b0 = fm_mm(wfm_d[FM_G + n], hT, "hT")
                b1 = fm_mm(wfm_d[FM_G + 8 + n], hT, "hT")
                g0, g0k = tf()
                ACT(g0, ps(b0), AF.Sigmoid, [pk(b0), "cf"], [g0k], bias=bfm(FM_G + n))
                g1, g1k = tf()
                ACT(g1, ps(b1), AF.Sigmoid, [pk(b1), "cf"], [g1k], bias=bfm(FM_G + 8 + n))
                TT(g0, ps(ba), g0, ALU.mult, [pk(ba), g0k], [g0k])
                TT(g1, ps(bm), g1, ALU.mult, [pk(bm), g1k], [g1k])
                TT(mgT[:, n, :], g0, g1, ALU.add, [g0k, g1k], ["aqT"])
            for n in range(8):
                b = fm_mm(wo_d[n], mgT, "aqT")
                TT(X["t"][:, n, :], ps(b), X["t"][:, n, :], ALU.add, [pk(b), xk(n)], [xk(n)])
                if hook is not None:
                    if n >= 2:
                        hook(n - 2)
                    if n == 7:
                        hook(6)
                        hook(7)

        def dump(name, tile_ap, key):
            if name in dbg_d:
                P.dma("sp", dbg_d[name], tile_ap, reads=[key], semkey="dbg_" + name)

        tgs = list(range(4 - n_tg_prev, 4)) + list(range(4, 4 + n_tg_own))
        hoisted = {"n0": False}
        n_prev_run = n_tg_prev
        for pos, tg in enumerate(tgs):
            own = tg >= 4
            P.epoch = pos + 1
            t = pos
            X["t"], X["i"] = xbuf[pos % 2], pos % 2
            if pos == 0:
                P.dma("sp", xbuf[0][:], xin[:, :, tg * 512:(tg + 1) * 512].rearrange("k p n -> p k n"),
                      writes=xall(), semkey=("xld", 0))
            if pos + 1 < len(tgs):
                ntg, ni = tgs[pos + 1], (pos + 1) % 2
                P.dma("sp", xbuf[ni][:], xin[:, :, ntg * 512:(ntg + 1) * 512].rearrange("k p n -> p k n"),
                      writes=[("x", ni, kc) for kc in range(8)], semkey=("xld", ni))
            if not hoisted["n0"]:
                norm(0, hT, "hT")
            hoisted["n0"] = False
            na = NormAcc()
            ffn(1, hook=na.add)
            if level >= 2:
                na.finish(1, hT, "hT")
                w_in(t, own)
                if (not own) and pos + 1 < len(tgs):
                    X["t"], X["i"] = xbuf[(pos + 1) % 2], (pos + 1) % 2
                    norm(0, hT, "hT")
                    X["t"], X["i"] = xbuf[pos % 2], pos % 2
                    hoisted["n0"] = True
            if own and level >= 3:
                attention(t)
            if level >= 4:
                mlstm(t, own)
                halo_copy()
            if own and level < 5:
                P.dma("sp", out_d[:, :, (tg - 4) * 512:(tg - 3) * 512].rearrange("k p n -> p k n"), X["t"][:],
                      reads=xall() + ["aqT", "ak2T", "av2", "moT", "qkT", "mv", "yaT", "hmT", "C"], semkey="out")
            if own and level >= 5:
                if tg == 4:
                    dump("ya", yaT[:].rearrange("p k n -> p (k n)"), "yaT")
                    dump("hm", hmT[:].rearrange("p k n -> p (k n)"), "hmT")
                    dump("qk", qkT[:].rearrange("p k n -> p (k n)"), "qkT")
                na = NormAcc()
                proj(hook=na.add)
                if tg == 4:
                    pass
                na.finish(2, hT, "hT")
                na = NormAcc()
                ffn(2, hook=na.add)
                na.finish(3, None, "xT")
                P.dma("sp", out_d[:, :, (tg - 4) * 512:(tg - 3) * 512].rearrange("k p n -> p k n"), X["t"][:],
                      reads=xall(), semkey="out")
        fw = ["out"] + ["dbg_" + n for n in dbg_d]
        P.emit(final_wait_keys=fw)
    return nc


def _tile_fm(W, cols=None):
    Wc = W if cols is None else W[:, cols]
    n = Wc.shape[1] // 128
    return np.ascontiguousarray(Wc.reshape(8, 128, n, 128).transpose(2, 1, 0, 3)).reshape(n, 128, 1024)


def _fm_cols():
    cols = [np.arange(AQ, AQ + 1024)]
    z = np.full(64, ZC)
    for g in range(4):
        c = np.arange(AK + g * 64, AK + g * 64 + 64)
        cols.append(np.concatenate([c, c]))
    for g in range(4):
        cols.append(np.concatenate([z, z]))
    cols.append(np.arange(MO, MO + 1024))
    for base in (MF, MI):
        for c in range(4):
            cols.append(np.concatenate([np.full(64, base + 2 * c), np.full(64, base + 2 * c + 1)]))
    cols.append(np.arange(MQ, MQ + 512))
    cols.append(np.arange(MK, MK + 512))
    cols.append(np.arange(GP, GP + 2048))
    return np.concatenate(cols)


def _tm_cols():
    cols = []
    for g in range(4):
        c = np.arange(AV + g * 64, AV + g * 64 + 64)
        cols.append(np.concatenate([c, c]))
    cols.append(np.arange(MV, MV + 1024))
    return np.concatenate(cols)


def _prep_common(inp):
    f = lambda a: np.ascontiguousarray(np.asarray(a, dtype=np.float32))
    w_in = np.concatenate([f(inp["w_in"])[0], np.zeros((1024, 1), np.float32)], axis=1)
    b_in = np.concatenate([f(inp["b_in"])[0], np.zeros(1, np.float32)])
    fmc, tmc = _fm_cols(), _tm_cols()
    com = {}
    for i, k in ((1, "ffn1"), (2, "ffn2")):
        com["wg%d" % i] = _tile_fm(f(inp[k + "_w_gate"])[0])
        com["wu%d" % i] = _tile_fm(f(inp[k + "_w_up"])[0])
        com["wd%d" % i] = np.ascontiguousarray(f(inp[k + "_w_down"])[0].reshape(NMF, 128, 1024))
    com["wfm"] = _tile_fm(w_in, fmc)
    wt = w_in[:, tmc]
    com["wtm"] = np.ascontiguousarray(wt.reshape(8, 128, 6, 256).transpose(2, 1, 0, 3)).reshape(6, 128, 2048)
    gcols = np.concatenate([np.arange(MF, MF + 8), np.arange(MI, MI + 8)])
    com["wgt"] = np.ascontiguousarray(w_in[:, gcols].reshape(8, 128, 16).transpose(1, 0, 2)).reshape(128, 128)
    com["wpa"] = _tile_fm(f(inp["w_proj_attn"])[0])
    com["wpm"] = _tile_fm(f(inp["w_proj_mlstm"])[0])
    com["wo"] = _tile_fm(f(inp["w_out"])[0])
    cf = np.zeros((128, NCF), np.float32)
    cf[:, C_BFM:C_BFM + NFM] = b_in[fmc].reshape(NFM, 128).T
    gains = [f(inp["ffn1_norm"])[0], f(inp["mix_norm"])[0], f(inp["ffn2_norm"])[0], f(inp["final_norm"])]
    for gi, g in enumerate(gains):
        cf[:, C_GAIN + gi * 8:C_GAIN + gi * 8 + 8] = g.reshape(8, 128).T
    cf[:, C_HN:C_HN + 8] = f(inp["mlstm_head_norm"])[0].T
    conv = f(inp["mlstm_conv"])[0]
    cf[:, C_CONV:C_CONV + 32] = conv.reshape(4, 8, 128).transpose(2, 1, 0).reshape(128, 32)
    sinks = f(inp["attn_sinks"])[0]
    perm = [4 * g + 2 * cc + bi for g in range(4) for bi in range(2) for cc in range(2)]
    cf[:, C_SINK:C_SINK + 16] = sinks[perm][None, :]
    cf[:, C_BTM:C_BTM + 1536] = b_in[tmc][None, :]
    sm = np.ones(512, np.float32)
    sm[::128] = 0.0
    cf[:, C_SCAN:C_SCAN + 512] = sm[None, :]
    cf[:, C_EPS] = 1e-6
    cf[:, C_ONE] = 1.0
    cf[:, C_LN8] = np.float32(np.log(0.125))
    cf[:, C_TINY] = 1e-30
    cf[0:8, C_BG] = b_in[MF:MF + 8]
    cf[0:8, C_BG + 1] = b_in[MI:MI + 8]
    for c in range(4):
        for p in range(128):
            cf[2 * c + p // 64, C_SEL + c * 128 + p] = 1.0
    k = np.arange(128)
    cb = np.zeros((128, 512), np.float32)
    cb[:, 0:128] = (k[:, None] <= k[None, :])
    cb[:, 128:256] = (k[:, None] > k[None, :])
    cb[:, 256:384] = np.eye(128, dtype=np.float32)
    cb[:, 384:512] = 1.0
    com["cb"] = cb
    return com, cf


def _core_inputs(x, com, cf, core):
    b, hf = core // 2, core % 2
    xin = np.zeros((1024, 4096), np.float32)
    if hf == 1:
        xin[:, 0:2048] = x[b, 0:2048].T
    xin[:, 2048:4096] = x[b, hf * 2048:(hf + 1) * 2048].T
    cfc = cf.copy()
    cfc[:, C_FLAG] = float(hf)
    d = dict(com)
    d["xT_in"] = np.ascontiguousarray(xin.reshape(8, 128, 4096))
    d["cf"] = cfc
    return d


def kernel(**inputs):
    x = np.asarray(inputs["x"], dtype=np.float32)
    com, cf = _prep_common(inputs)
    nc = build_program()
    in_maps = [_core_inputs(x, com, cf, c) for c in range(8)]
    res = run_bass_kernel_spmd(nc, in_maps, core_ids=list(range(8)))
    out = np.empty((4, 4096, 1024), np.float32)
    for c in range(8):
        o = np.asarray(res.results[c]["outT"]).reshape(1024, 2048)
        out[c // 2, (c % 2) * 2048:(c % 2 + 1) * 2048, :] = o.T
    return out
```

```python
import contextlib
import numpy as np
import concourse.bass as bass
import concourse.mybir as mybir
from concourse.bass_utils import run_bass_kernel_spmd

F32 = mybir.dt.float32
BF16 = mybir.dt.bfloat16
AF = mybir.ActivationFunctionType
ALU = mybir.AluOpType
ENGS = ("pe", "act", "dve", "pool", "sp")

D_MODEL = 1024
D_FF = 2816
NMF = D_FF // 128
AQ, AK, AV, MQ, MK, MV, MO, MI, MF, GP = 0, 1024, 1280, 1536, 2048, 2560, 3584, 4608, 4616, 4624
NFM = 56
FM_AQ, FM_AK, FM_MO, FM_F, FM_I, FM_MQ, FM_MK, FM_G = 0, 8, 16, 24, 28, 32, 36, 40
ZC = 6672
C_BFM, C_GAIN, C_HN, C_CONV, C_SINK, C_FLAG, C_BTM, C_SCAN, C_EPS, C_ONE, C_LN8, C_TINY, C_BG, C_SEL = (
    0, 56, 88, 96, 128, 144, 145, 1681, 2193, 2194, 2195, 2196, 2197, 2199)
NCF = 2199 + 512
import os
ATT_SUB = int(os.environ.get('ATT_SUB', '9'))


class Op:
    __slots__ = ("eng", "fn", "reads", "writes", "dma", "deps", "signal", "cnt", "semkey", "idx", "epoch", "eseq")


class Prog:
    def __init__(self, nc):
        self.nc = nc
        self.ops = []
        self.last_w = {}
        self.readers = {}
        self.epoch = 0
        self.eseq = {e: 0 for e in ENGS}

    def op(self, eng, fn, reads=(), writes=(), dma=False, semkey=None):
        o = Op()
        o.eng, o.fn, o.reads, o.writes, o.dma, o.semkey = eng, fn, tuple(reads), tuple(writes), dma, semkey
        o.signal, o.cnt, o.epoch = False, 0, self.epoch
        o.idx = len(self.ops)
        o.eseq = self.eseq[eng]
        self.eseq[eng] += 1
        raw = set()
        deps = set()
        for k in o.reads:
            w = self.last_w.get(k)
            if w is not None:
                deps.add(w)
                raw.add(w)
        for k in o.writes:
            w = self.last_w.get(k)
            if w is not None:
                deps.add(w)
            deps.update(self.readers.get(k, {}).values())
        need = []
        for d in deps:
            dop = self.ops[d]
            if dop.dma or dop.eng != eng:
                need.append(d)
            elif eng != "pe" and not dma:
                if d in raw or (o.eseq - dop.eseq) <= 3:
                    need.append(d)
        o.deps = need
        for d in need:
            self.ops[d].signal = True
        for k in o.reads:
            rd = self.readers.setdefault(k, {})
            rd[("dma", o.idx) if dma else eng] = o.idx
        for k in o.writes:
            self.last_w[k] = o.idx
            self.readers[k] = {}
        self.ops.append(o)
        return o

    def dma(self, q, out, in_, reads=(), writes=(), semkey=None):
        return self.op(q, lambda e: e.dma_start(out=out, in_=in_), reads, writes, dma=True, semkey=semkey)

    def emit(self, final_wait_keys=()):
        nc = self.nc
        eng_cnt = {}
        dma_cnt = {}
        for o in self.ops:
            if o.dma:
                dma_cnt[o.semkey] = dma_cnt.get(o.semkey, 0) + 16
                o.cnt = dma_cnt[o.semkey]
            elif o.signal:
                ek = (o.eng, o.epoch)
                eng_cnt[ek] = eng_cnt.get(ek, 0) + 1
                o.cnt = eng_cnt[ek]
        semkeys = sorted(dma_cnt.keys(), key=str)
        with contextlib.ExitStack() as st:
            esem = {ek: st.enter_context(nc.semaphore("s_%s_%d" % ek)) for ek in sorted(eng_cnt.keys())}
            dsem = {k: st.enter_context(nc.semaphore("d_%d" % i)) for i, k in enumerate(semkeys)}
            block = st.enter_context(nc.Block())
            ops = self.ops

            def run(engname, eng):
                waited = {}
                for o in ops:
                    if o.eng != engname:
                        continue
                    for d in o.deps:
                        dop = ops[d]
                        if dop.dma:
                            sem, val, key = dsem[dop.semkey], dop.cnt, ("d", dop.semkey)
                        else:
                            ek = (dop.eng, dop.epoch)
                            sem, val, key = esem[ek], dop.cnt, ("e", ek)
                        if waited.get(key, 0) >= val:
                            continue
                        waited[key] = val
                        eng.wait_ge(sem, val)
                    ins = o.fn(eng)
                    if o.dma:
                        ins.then_inc(dsem[o.semkey], 16)
                    elif o.signal:
                        ins.then_inc(esem[(o.eng, o.epoch)], 1)
                if engname == "sp":
                    for k in final_wait_keys:
                        eng.wait_ge(dsem[k], dma_cnt[k])

            block.tensor(lambda e: run("pe", e))
            block.scalar(lambda e: run("act", e))
            block.vector(lambda e: run("dve", e))
            block.gpsimd(lambda e: run("pool", e))
            block.sync(lambda e: run("sp", e))


def build_program(n_tg_prev=4, n_tg_own=4, dbg=None, level=9):
    nc = bass.Bass("TRN2", target_bir_lowering=False)

    def DI(name, shape):
        return nc.dram_tensor(name, shape, F32, kind="ExternalInput").ap()

    xin = DI("xT_in", [8, 128, 4096])
    cf_d = DI("cf", [128, NCF])
    cb_d = DI("cb", [128, 512])
    wffn = {1: (DI("wg1", [NMF, 128, 1024]), DI("wu1", [NMF, 128, 1024]), DI("wd1", [NMF, 128, 1024])),
            2: (DI("wg2", [NMF, 128, 1024]), DI("wu2", [NMF, 128, 1024]), DI("wd2", [NMF, 128, 1024]))}
    wfm_d = DI("wfm", [NFM, 128, 1024])
    wtm_d = DI("wtm", [6, 128, 2048])
    wgt_d = DI("wgt", [128, 128])
    wpa_d = DI("wpa", [8, 128, 1024])
    wpm_d = DI("wpm", [8, 128, 1024])
    wo_d = DI("wo", [8, 128, 1024])
    out_d = nc.dram_tensor("outT", [8, 128, 2048], F32, kind="ExternalOutput").ap()
    dbg_d = {}
    if dbg:
        for name, shape in dbg.items():
            dbg_d[name] = nc.dram_tensor("dbg_" + name, shape, F32, kind="ExternalOutput").ap()

    with contextlib.ExitStack() as st:
        def SB(name, shape, dt):
            return st.enter_context(nc.sbuf_tensor("sb_" + name, shape, dt))

        P = Prog(nc)
        xbuf = [SB("xT%d" % i, [128, 8, 512], F32) for i in range(2)]
        X = {"t": xbuf[0], "i": 0}

        def xk(kc):
            return ("x", X["i"], kc)

        def xall():
            return [("x", X["i"], kc) for kc in range(8)]
        hT = SB("hT", [128, 8, 512], BF16)
        MG = 4
        aT = SB("aT", [128, MG, 512], BF16)
        cf = SB("cfs", [128, NCF], F32)
        cb = SB("cbs", [128, 512], BF16)
        aqT = SB("aqT", [128, 8, 512], BF16)
        ak2T = SB("ak2T", [128, 8, 640], BF16)
        kz = SB("kz", [128, 8, 512], BF16)
        av2 = SB("av2", [128, 5, 512], BF16)
        moT = SB("moT", [128, 8, 512], BF16)
        qkT = SB("qkT", [128, 4, 512], BF16)
        mv = SB("mv", [128, 4, 8, 130], BF16)
        yaT = SB("yaT", [128, 8, 512], BF16)
        hmT = SB("hmT", [128, 8, 512], BF16)
        mgT = aqT
        Cst = SB("Cst", [128, 4, 129], F32)
        Cb = SB("Cb", [128, 2, 8, 128], BF16)
        ksum = SB("ksum", [128, 2, 8], F32)
        Nb = SB("Nb", [128, 2, 8, 128], BF16)
        eg = SB("eg", [128, 4, 4], F32)
        halo = SB("halo", [128, 8, 3], F32)
        b15 = SB("b15", [128, 8], F32)
        geb = SB("geb", [8, 512], F32)
        gek = SB("gek", [8, 512], F32)
        esink = SB("esink", [128, 16], F32)
        mpf = SB("mpf", [128, 128], BF16)
        mbp = SB("mbp", [128, 512], BF16)
        mbc = SB("mbc", [128, 512], BF16)
        mbf = SB("mbf", [128, 512], BF16)
        esrow = SB("esrow", [1, 2048], BF16)
        pre = [SB("pre%d" % i, [128, 515], F32) for i in range(2)]
        rings = {
            "w": [SB("w%d" % i, [128, 1024], BF16) for i in range(8)],
            "wd": [SB("wd%d" % i, [128, 1024], BF16) for i in range(2 * MG)],
            "tf": [SB("tf%d" % i, [128, 512], F32) for i in range(8)],
            "tb": [SB("tb%d" % i, [128, 512], BF16) for i in range(8)],
        }
        rpos = {k: 0 for k in rings}
        psum = st.enter_context(nc.psum_tensor("psum", [128, 4096], F32))
        pstate = {"i": 0}

        live_banks = set()

        def nb():
            for _ in range(7):
                b = pstate["i"] % 7
                pstate["i"] += 1
                if b not in live_banks:
                    live_banks.add(b)
                    return b
            raise AssertionError("all PSUM banks live")

        _orig_op = P.op

        def _op(eng, fn, reads=(), writes=(), dma=False, semkey=None):
            for k in reads:
                if isinstance(k, tuple) and k[0] == "ps":
                    live_banks.discard(k[1])
            return _orig_op(eng, fn, reads, writes, dma, semkey)

        P.op = _op

        def ps(b):
            return psum[:, b * 512:(b + 1) * 512]

        def pk(b):
            return ("ps", b)

        def rnext(name):
            i = rpos[name] % len(rings[name])
            rpos[name] += 1
            return rings[name][i], (name, i)

        def tf():
            t, k = rnext("tf")
            return t[:], k

        def tb():
            t, k = rnext("tb")
            return t[:], k

        def ring_load(name, src):
            t, k = rnext(name)
            P.dma("pool", t[:], src, writes=[k], semkey=k)
            return t, k

        def MM(out, lhsT, rhs, start, stop, reads, writes):
            P.op("pe", lambda e: e.matmul(out, lhsT=lhsT, rhs=rhs, start=start, stop=stop), reads, writes)

        def ACT(out, in_, func, reads, writes, bias=None, scale=1.0):
            if bias is None:
                P.op("act", lambda e: e.activation(out=out, in_=in_, func=func, scale=scale), reads, writes)
            else:
                P.op("act", lambda e: e.activation(out=out, in_=in_, func=func, bias=bias, scale=scale), reads, writes)

        def TT(out, in0, in1, op, reads, writes, eng="dve"):
            P.op(eng, lambda e: e.tensor_tensor(out=out, in0=in0, in1=in1, op=op), reads, writes)

        def TS(out, in0, s1, op0, reads, writes, eng="dve"):
            P.op(eng, lambda e: e.tensor_scalar(out=out, in0=in0, scalar1=s1, scalar2=None, op0=op0), reads, writes)

        def STT(out, in0, scalar, in1, op0, op1, reads, writes, eng="dve"):
            P.op(eng, lambda e: e.scalar_tensor_tensor(out=out, in0=in0, scalar=scalar, in1=in1, op0=op0, op1=op1),
                 reads, writes)

        def TC(out, in_, reads, writes, eng="dve"):
            P.op(eng, lambda e: e.tensor_copy(out=out, in_=in_), reads, writes)

        def RCP(out, in_, reads, writes):
            P.op("dve", lambda e: e.reciprocal(out=out, in_=in_), reads, writes)

        def r4(ap):
            return ap.rearrange("p (a b) -> p a b", a=4)

        def r2(ap):
            return ap.rearrange("p (a b) -> p a b", a=2)

        P.dma("sp", cf[:], cf_d, writes=["cf"], semkey="cf")
        P.dma("pool", cb[:], cb_d, writes=["cb"], semkey="cb")
        mask_cur, mask_prev, ident, ones = cb[:, 0:128], cb[:, 128:256], cb[:, 256:384], cb[:, 384:512]
        flag = cf[:, C_FLAG:C_FLAG + 1]
        eps = cf[:, C_EPS:C_EPS + 1]
        one = cf[:, C_ONE:C_ONE + 1]
        ln8 = cf[:, C_LN8:C_LN8 + 1]
        tiny = cf[:, C_TINY:C_TINY + 1]
        scanmask = cf[:, C_SCAN:C_SCAN + 512]
        btm = cf[:, C_BTM:C_BTM + 1536]

        def bfm(ci):
            return cf[:, C_BFM + ci:C_BFM + ci + 1]

        TS(b15[0:8, 0:2], cf[0:8, C_BG:C_BG + 2], 1.0 / 15.0, ALU.mult, ["cf"], ["b15"])
        ACT(esink[:], cf[:, C_SINK:C_SINK + 16], AF.Exp, ["cf"], ["esink"])
        TS(mpf[:], mask_prev, flag, ALU.mult, ["cb", "cf"], ["mpf"])
        for (dst, src, k_) in ((mbp, mask_prev, "cb"), (mbc, mask_cur, "cb"), (mbf, mpf[:], "mpf")):
            P.op("dve", lambda e, dst=dst, src=src: e.tensor_scalar(
                out=r4(dst[:]), in0=src.unsqueeze(1).to_broadcast([128, 4, 128]), scalar1=-1.0, scalar2=30000.0,
                op0=ALU.add, op1=ALU.mult), [k_], ["mbp" if dst is not mbf else "mbf"])
        TC(esrow[0:1, :].rearrange("p (a b) -> p a b", a=16), esink[0:1, :].unsqueeze(2).to_broadcast([1, 16, 128]),
           ["esink"], ["esrow"])
        P.op("dve", lambda e: e.memset(halo[:], 0.0), writes=["halo"])
        P.op("dve", lambda e: e.memset(Cst[:], 0.0), writes=["C"])
        P.op("dve", lambda e: e.memset(Cb[:], 0.0), writes=[("Cb", 0), ("Cb", 1)])
        P.op("dve", lambda e: e.memset(Nb[:], 0.0), writes=[("Cb", 0), ("Cb", 1)])
        P.op("dve", lambda e: e.memset(mv[:], 1.0), writes=["mv"])
        P.op("dve", lambda e: e.memset(ak2T[:], 0.0), writes=["ak2T"])
        P.op("dve", lambda e: e.memset(kz[:], 0.0), writes=["kz"])
        P.op("dve", lambda e: e.memset(av2[:], 0.0), writes=["av2"])

        class NormAcc:
            def __init__(self):
                self.b = 7
                self.n = 0

            def add(self, kc):
                sq, sqk = tb()
                ACT(sq, X["t"][:, kc, :], AF.Square, [xk(kc)], [sqk])
                MM(ps(self.b), ones, sq, self.n == 0, self.n == 7, [sqk, "cb"], [pk(self.b)])
                self.n += 1

            def finish(self, gi, dst, dkey):
                assert self.n == 8
                b = self.b
                rs, rsk = tf()
                ACT(rs, ps(b), AF.Ln, [pk(b), "cf"], [rsk], bias=eps, scale=1.0 / D_MODEL)
                ACT(rs, rs, AF.Exp, [rsk], [rsk], scale=-0.5)
                for kc in range(8):
                    g = cf[:, C_GAIN + gi * 8 + kc:C_GAIN + gi * 8 + kc + 1]
                    dk = (dkey, kc) if dkey == "hT" else xk(kc)
                    d_ = dst if dkey == "hT" else X["t"]
                    STT(d_[:, kc, :], X["t"][:, kc, :], g, rs, ALU.mult, ALU.mult, [xk(kc), rsk, "cf"], [dk])

        def norm(gi, dst, dkey):
            na = NormAcc()
            for kc in range(8):
                na.add(kc)
            na.finish(gi, dst, dkey)

        def ffn(which, hook=None):
            wg_d, wu_d, wd_d = wffn[which]
            m0 = 0
            while m0 < NMF:
                grp = list(range(m0, min(m0 + MG, NMF)))
                m0 += MG
                wds = []
                for j, m in enumerate(grp):
                    wg_t, wg_k = ring_load("w", wg_d[m])
                    wu_t, wu_k = ring_load("w", wu_d[m])
                    wds.append(ring_load("wd", wd_d[m]))
                    bg, bu = nb(), nb()
                    for kc in range(8):
                        MM(ps(bg), wg_t[:, kc * 128:(kc + 1) * 128], hT[:, kc, :], kc == 0, kc == 7,
                           [wg_k, ("hT", kc)], [pk(bg)])
                    for kc in range(8):
                        MM(ps(bu), wu_t[:, kc * 128:(kc + 1) * 128], hT[:, kc, :], kc == 0, kc == 7,
                           [wu_k, ("hT", kc)], [pk(bu)])
                    s, sk = tf()
                    ACT(s, ps(bg), AF.Silu, [pk(bg)], [sk])
                    TT(aT[:, j, :], s, ps(bu), ALU.mult, [sk, pk(bu)], [("aT", j)])
                for n in range(8):
                    by = nb()
                    for j in range(len(grp)):
                        wd_t, wd_k = wds[j]
                        MM(ps(by), wd_t[:, n * 128:(n + 1) * 128], aT[:, j, :], j == 0, j == len(grp) - 1,
                           [wd_k, ("aT", j)], [pk(by)])
                    STT(X["t"][:, n, :], ps(by), 0.5, X["t"][:, n, :], ALU.mult, ALU.add, [pk(by), xk(n)], [xk(n)])
                    if hook is not None and m0 >= NMF:
                        if n >= 2:
                            hook(n - 2)
                        if n == 7:
                            hook(6)
                            hook(7)

        def fm_mm(src, rhs_t, rhs_key):
            wt, wk = ring_load("w", src)
            b = nb()
            for kc in range(8):
                rk_ = (rhs_key, kc) if rhs_key == "hT" else rhs_key
                MM(ps(b), wt[:, kc * 128:(kc + 1) * 128], rhs_t[:, kc, :], kc == 0, kc == 7, [wk, rk_], [pk(b)])
            return b

        def w_in(t, own):
            first_own = (t == n_tg_prev)
            need_halo = own or (t == n_tg_prev - 1)
            tm_todo = [gi for gi in range(6) if (gi >= 2 or need_halo)]

            def tm_group():
                if not tm_todo:
                    return
                gi = tm_todo.pop(0)
                halves = [ring_load("w", wtm_d[gi][:, hf_ * 1024:(hf_ + 1) * 1024]) for hf_ in range(2)]
                for blk in range(4):
                    b = nb()
                    for kc in range(8):
                        wt, wk = halves[kc // 4]
                        MM(ps(b)[:, 0:256], hT[:, kc, blk * 128:(blk + 1) * 128], wt[:, (kc % 4) * 256:(kc % 4 + 1) * 256],
                           kc == 0, kc == 7, [wk, ("hT", kc)], [pk(b)])
                    bias = btm[:, gi * 256:(gi + 1) * 256]
                    if gi < 2:
                        TT(av2[:, 1 + blk, gi * 256:(gi + 1) * 256], ps(b)[:, 0:256], bias, ALU.add, [pk(b), "cf"], ["av2"])
                    else:
                        h0 = (gi - 2) * 2
                        TT(mv[:, blk, h0:h0 + 2, 0:128], r2(ps(b)[:, 0:256]), r2(bias), ALU.add, [pk(b), "cf"], ["mv"])

            if own:
                for c in range(8):
                    b = fm_mm(wfm_d[c], hT, "hT")
                    ACT(aqT[:, c, :], ps(b), AF.Identity, [pk(b), "cf"], ["aqT"], bias=bfm(c))
                    if c == 3:
                        tm_group()
            if need_halo:
                tm_group()
            for g in range(4):
                if not need_halo:
                    break
                b = fm_mm(wfm_d[FM_AK + g], hT, "hT")
                ACT(ak2T[0:64, 2 * g, 128:640], ps(b)[0:64, :], AF.Identity, [pk(b), "cf"], ["ak2T"],
                    bias=cf[0:64, C_BFM + FM_AK + g:C_BFM + FM_AK + g + 1])
                ACT(ak2T[64:128, 2 * g + 1, 128:640], ps(b)[64:128, :], AF.Identity, [pk(b), "cf"], ["ak2T"],
                    bias=cf[64:128, C_BFM + FM_AK + g:C_BFM + FM_AK + g + 1])
            if own:
                for h in range(8):
                    b = fm_mm(wfm_d[FM_MO + h], hT, "hT")
                    s, sk = tf()
                    ACT(s, ps(b), AF.Sigmoid, [pk(b), "cf"], [sk], bias=bfm(FM_MO + h))
                    TS(moT[:, h, :], s, cf[:, C_HN + h:C_HN + h + 1], ALU.mult, [sk, "cf"], ["moT"])
            need_q = own or (t == n_tg_prev - 1)
            wg_t, wg_k = rnext("w")
            P.dma("pool", wg_t[:, 0:128], wgt_d, writes=[wg_k], semkey=wg_k)
            b_f, b_i = nb(), nb()
            for kc in range(8):
                MM(ps(b_f)[0:8, :], wg_t[:, kc * 16:kc * 16 + 8], hT[:, kc, :], kc == 0, kc == 7,
                   [wg_k, ("hT", kc)], [pk(b_f)])
            for kc in range(8):
                MM(ps(b_i)[0:8, :], wg_t[:, kc * 16 + 8:kc * 16 + 16], hT[:, kc, :], kc == 0, kc == 7,
                   [wg_k, ("hT", kc)], [pk(b_i)])
            t1_, t1k = tf()
            t4_, t4k = tf()
            bt_, btk = tf()
            t1, t4, bt = t1_[0:8, :], t4_[0:8, :], bt_[0:8, :]
            ACT(t1, ps(b_f)[0:8, :], AF.Tanh, [pk(b_f), "b15"], [t1k], bias=b15[0:8, 0:1], scale=1.0 / 15.0)
            ACT(t4, ps(b_i)[0:8, :], AF.Tanh, [pk(b_i), "b15"], [t4k], bias=b15[0:8, 1:2], scale=1.0 / 15.0)
            ACT(t1, t1, AF.Exp, [t1k], [t1k], scale=-15.0)
            ACT(t1, t1, AF.Ln, [t1k, "cf"], [t1k], bias=one[0:8, :])
            P.op("dve", lambda e: e.tensor_tensor_scan(
                out=bt, data0=scanmask[0:8, :], data1=t1, initial=0.0, op0=ALU.mult, op1=ALU.subtract),
                [t1k, "cf"], [btk])
            ACT(geb[0:8, :], bt, AF.Exp, [btk], ["geb"])
            STT(t4, t4, 15.0, bt, ALU.mult, ALU.subtract, [t4k, btk], [t4k])
            ACT(gek[0:8, :], t4, AF.Exp, [t4k, "cf"], ["gek"], bias=ln8[0:8, :])
            for c in range(4):
                tm_group_pair = None
                chunks = []
                for (ci, qc) in ((FM_MQ + c, c), (FM_MK + c, 4 + c)):
                    if qc < 4 and not need_q:
                        continue
                    chunks.append((ci, qc, fm_mm(wfm_d[ci], hT, "hT")))
                pres = []
                for k_, (ci, qc, b) in enumerate(chunks):
                    pi = rpos.setdefault("pre", 0) % 2
                    rpos["pre"] += 1
                    pt, pkk = pre[pi], ("pre", pi)
                    ACT(pt[:, 3:515], ps(b), AF.Identity, [pk(b), "cf"], [pkk], bias=bfm(ci))
                    pres.append((pt, pkk))
                accs = []
                for k_, (ci, qc, b) in enumerate(chunks):
                    pt, pkk = pres[k_]
                    if first_own:
                        TS(pt[:, 0:3], halo[:, qc, :], flag, ALU.mult, ["halo", "cf"], [pkk])
                    else:
                        TC(pt[:, 0:3], halo[:, qc, :], ["halo"], [pkk])
                    acc, ack = tf()
                    cw = C_CONV + qc * 4
                    TS(acc, pt[:, 0:512], cf[:, cw:cw + 1], ALU.mult, [pkk, "cf"], [ack])
                    for j in range(1, 4):
                        STT(acc, pt[:, j:j + 512], cf[:, cw + j:cw + j + 1], acc, ALU.mult, ALU.add,
                            [pkk, "cf", ack], [ack])
                    TC(halo[:, qc, :], pt[:, 512:515], [pkk], ["halo"])
                    accs.append((acc, ack, qc))
                for (acc, ack, qc) in accs:
                    ACT(acc, acc, AF.Silu, [ack], [ack])
                selc = cf[0:8, C_SEL + c * 128:C_SEL + (c + 1) * 128]
                b_eb = nb()
                MM(ps(b_eb), selc, geb[0:8, :], True, True, ["cf", "geb"], [pk(b_eb)])
                b_ek = nb()
                MM(ps(b_ek), selc, gek[0:8, :], True, True, ["cf", "gek"], [pk(b_ek)])
                TC(eg[:, c, :], r4(ps(b_eb))[:, :, 127], [pk(b_eb)], ["eg"])
                for (acc, ack, qc) in accs:
                    if qc < 4:
                        TT(qkT[:, qc, :], acc, ps(b_eb), ALU.mult, [ack, pk(b_eb)], ["qkT"])
                for (acc, ack, qc) in accs:
                    if qc >= 4:
                        TT(kz[0:64, 2 * c, :], acc[0:64, :], ps(b_ek)[0:64, :], ALU.mult, [ack, pk(b_ek)], ["kz"])
                        TT(kz[64:128, 2 * c + 1, :], acc[64:128, :], ps(b_ek)[64:128, :], ALU.mult, [ack, pk(b_ek)], ["kz"])
            while tm_todo:
                tm_group()
        def attention(t):
            first_own = (t == n_tg_prev)
            items = [(blk, g) for blk in range(4) for g in range(4)]
            Ems = {}

            def stage_a(blk, g):
                qs = slice(blk * 128, (blk + 1) * 128)
                Em = []
                for kb in range(2):
                    b = nb()
                    ks = slice((blk + kb) * 128, (blk + kb + 1) * 128)
                    if kb == 0:
                        mb_, mk_ = (mbf[:], "mbf") if (first_own and blk == 0) else (mbp[:], "mbp")
                    else:
                        mb_, mk_ = mbc[:], "mbp"
                    MM(ps(b), ident, mb_, True, False, ["cb", mk_], [pk(b)])
                    for bi in range(2):
                        MM(r2(ps(b)[:, bi * 256:(bi + 1) * 256]), ak2T[:, 2 * g + bi, ks], aqT[:, 2 * g:2 * g + 2, qs],
                           False, bi == 1, ["ak2T", "aqT"], [pk(b)])
                    E, Ek = tb()
                    ACT(E, ps(b), AF.Exp, [pk(b)], [Ek], scale=0.125)
                    Em.append((E, Ek))
                Ems[(blk, g)] = Em

            def stage_b(blk, g):
                qs = slice(blk * 128, (blk + 1) * 128)
                Em = Ems.pop((blk, g))
                bo, bd = nb(), nb()
                for kb in range(2):
                    MM(ps(bo), av2[:, blk + kb, g * 128:(g + 1) * 128], Em[kb][0], kb == 0, kb == 1,
                       ["av2", Em[kb][1]], [pk(bo)])
                for kb in range(2):
                    MM(ps(bd), ones, Em[kb][0], kb == 0, False, ["cb", Em[kb][1]], [pk(bd)])
                MM(ps(bd), ones[0:1, :], esrow[0:1, g * 512:(g + 1) * 512], False, True, ["cb", "esrow"], [pk(bd)])
                r, rk = tf()
                ACT(r, ps(bd), AF.Ln, [pk(bd)], [rk])
                ACT(r, r, AF.Exp, [rk], [rk], scale=-1.0)
                for bi in range(2):
                    rs_ = slice(bi * 64, bi * 64 + 64)
                    cs = slice(bi * 256, (bi + 1) * 256)
                    TT(yaT[rs_, 2 * g:2 * g + 2, qs], r2(ps(bo)[rs_, cs]), r2(r[rs_, cs]), ALU.mult,
                       [pk(bo), rk], ["yaT"])

            stage_a(*items[0])
            for i, it in enumerate(items):
                if i + 1 < len(items):
                    stage_a(*items[i + 1])
                stage_b(*it)

        def halo_copy():
            TC(ak2T[:, :, 0:128], ak2T[:, :, 512:640], ["ak2T"], ["ak2T"])
            TC(av2[:, 0, :], av2[:, 4, :], ["av2"], ["av2"])

        CbV = [[Cb[hh_ * 64:(hh_ + 1) * 64, v].rearrange("p (c h) w -> p c h w", h=2)[:, :, hh_, :] for hh_ in range(2)]
               for v in range(2)]
        NbV = [[Nb[hh_ * 64:(hh_ + 1) * 64, v].rearrange("p (c h) w -> p c h w", h=2)[:, :, hh_, :] for hh_ in range(2)]
               for v in range(2)]
        mst = {"g": 0, "ks": 0}

        def recast_state(v):
            for hh_ in range(2):
                hs = slice(hh_ * 64, hh_ * 64 + 64)
                TC(CbV[v][hh_], Cst[hs, :, 0:128], ["C"], [("Cb", v)])
                TC(NbV[v][hh_], Cst[hs, :, 128:129].to_broadcast([64, 4, 128]), ["C"], [("Cb", v)])

        def mlstm(t, own):
            if t == n_tg_prev:
                TS(Cst[:], Cst[:], flag, ALU.mult, ["C", "cf"], ["C"])
                recast_state(mst["g"] % 2)
            g0 = mst["g"]
            mst["g"] += 4
            Ub, Sm, post = {}, {}, {}

            def tu_mm(blk):
                qs = slice(blk * 128, (blk + 1) * 128)
                bK = nb()
                for c in range(4):
                    for hh_ in range(2):
                        MM(ps(bK)[:, c * 128:(c + 1) * 128], kz[:, 2 * c + hh_, qs], ident, hh_ == 0, hh_ == 1,
                           ["kz", "cb"], [pk(bK)])
                kt, ktk = tb()
                ACT(kt, ps(bK), AF.Identity, [pk(bK)], [ktk])
                bU = [nb(), nb()]
                for c in range(4):
                    MM(r2(ps(bU[c // 2])[:, (c % 2) * 256:(c % 2) * 256 + 256]), kt[:, c * 128:(c + 1) * 128],
                       mv[:, blk, 2 * c:2 * c + 2, 0:128], True, True, [ktk, "mv"], [pk(bU[c // 2])])
                Ub[blk] = bU

            def upd(blk):
                qs = slice(blk * 128, (blk + 1) * 128)
                bU = Ub.pop(blk)
                ki = mst["ks"] % 2
                mst["ks"] += 1
                P.op("dve", lambda e: e.reduce_sum(out=ksum[:, ki, :], in_=kz[:, :, qs], axis=mybir.AxisListType.X),
                     ["kz"], [("ks", ki)])
                for b_ in range(2):
                    for hh_ in range(2):
                        hs = slice(hh_ * 64, hh_ * 64 + 64)
                        uv = ps(bU[b_])[hs, :].rearrange("p (c h w) -> p c h w", c=2, h=2)[:, :, hh_, :]
                        TT(Cst[hs, 2 * b_:2 * b_ + 2, 0:128], uv, Cst[hs, 2 * b_:2 * b_ + 2, 0:128], ALU.add,
                           [pk(bU[b_]), "C"], ["C"])
                ks2 = ksum[:, ki, :].rearrange("p (c h) -> p c h", h=2)
                TT(Cst[:, :, 128], Cst[:, :, 128], ks2[:, :, 0], ALU.add, ["C", ("ks", ki)], ["C"])
                TT(Cst[:, :, 128], Cst[:, :, 128], ks2[:, :, 1], ALU.add, ["C", ("ks", ki)], ["C"])
                TT(Cst[:], Cst[:], eg[:, :, blk:blk + 1].to_broadcast([128, 4, 129]), ALU.mult, ["C", "eg"], ["C"])
                recast_state((g0 + blk + 1) % 2)

            def s_stage(blk):
                qs = slice(blk * 128, (blk + 1) * 128)
                bS = [nb(), nb()]
                for h in range(8):
                    c = h // 2
                    cs = slice((h % 4) * 128, (h % 4) * 128 + 128)
                    MM(ps(bS[h // 4])[:, cs], kz[:, h, qs], qkT[:, c, qs], True, True, ["qkT", "kz"], [pk(bS[h // 4])])
                sm = []
                for i in range(2):
                    s_, sk_ = tb()
                    TT(r4(s_), r4(ps(bS[i])), mask_cur.unsqueeze(1).to_broadcast([128, 4, 128]), ALU.mult,
                       [pk(bS[i]), "cb"], [sk_])
                    sm.append((s_, sk_))
                Sm[blk] = sm

            def nd_stage(blk):
                qs = slice(blk * 128, (blk + 1) * 128)
                v = (g0 + blk) % 2
                sm = Sm.pop(blk)
                bN = [nb(), nb()]
                bD = [nb(), nb()]
                for h in range(8):
                    c = h // 2
                    cs = slice((h % 4) * 128, (h % 4) * 128 + 128)
                    i = h // 4
                    MM(ps(bN[i])[:, cs], mv[:, blk, h, 0:128], sm[i][0][:, cs], True, False, ["mv", sm[i][1]], [pk(bN[i])])
                    MM(ps(bN[i])[:, cs], Cb[:, v, h, :], qkT[:, c, qs], False, True, [("Cb", v), "qkT"], [pk(bN[i])])
                    MM(ps(bD[i])[:, cs], ones, sm[i][0][:, cs], True, False, ["cb", sm[i][1]], [pk(bD[i])])
                    MM(ps(bD[i])[:, cs], Nb[:, v, h, :], qkT[:, c, qs], False, True, [("Cb", v), "qkT"], [pk(bD[i])])
                A = [tf() for _ in range(2)]
                H = [tf() for _ in range(2)]
                Q = [tb() for _ in range(2)]
                for i in range(2):
                    ACT(A[i][0], ps(bD[i]), AF.Abs, [pk(bD[i])], [A[i][1]])
                for i in range(2):
                    ACT(A[i][0], A[i][0], AF.Ln, [A[i][1], "cf"], [A[i][1]], bias=tiny)
                for i in range(2):
                    ACT(A[i][0], A[i][0], AF.Exp, [A[i][1]], [A[i][1]], scale=-1.0)
                for i in range(2):
                    STT(H[i][0], A[i][0], 1.0, ps(bN[i]), ALU.min, ALU.mult, [A[i][1], pk(bN[i])], [H[i][1]])
                for i in range(2):
                    ACT(Q[i][0], H[i][0], AF.Square, [H[i][1]], [Q[i][1]])
                post[blk] = [(H[i][0], H[i][1], Q[i][0], Q[i][1]) for i in range(2)]

            def ssq_stage(blk):
                qs = slice(blk * 128, (blk + 1) * 128)
                pp = post.pop(blk)
                bqs = []
                for i, (hh, hk, sq, sqk) in enumerate(pp):
                    bq = nb()
                    MM(ps(bq), ones, sq, True, True, ["cb", sqk], [pk(bq)])
                    bqs.append(bq)
                R = [tf() for _ in range(2)]
                for i in range(2):
                    ACT(R[i][0], ps(bqs[i]), AF.Ln, [pk(bqs[i]), "cf"], [R[i][1]], bias=eps, scale=1.0 / 128.0)
                for i in range(2):
                    ACT(R[i][0], R[i][0], AF.Exp, [R[i][1]], [R[i][1]], scale=-0.5)
                for i, (hh, hk, sq, sqk) in enumerate(pp):
                    TT(hh, hh, R[i][0], ALU.mult, [hk, R[i][1]], [hk])
                for i, (hh, hk, sq, sqk) in enumerate(pp):
                    TT(hmT[:, 4 * i:4 * i + 4, qs], r4(hh), moT[:, 4 * i:4 * i + 4, qs], ALU.mult,
                       [hk, "moT"], ["hmT"])

            if not own:
                for blk in range(4):
                    tu_mm(blk)
                    upd(blk)
                return
            tu_mm(0); upd(0); s_stage(0)
            tu_mm(1); s_stage(1)
            nd_stage(0); upd(1)
            tu_mm(2); s_stage(2)
            ssq_stage(0)
            nd_stage(1); upd(2)
            tu_mm(3); s_stage(3)
            ssq_stage(1)
            nd_stage(2); upd(3)
            ssq_stage(2)
            nd_stage(3)
            ssq_stage(3)

        def proj(hook=None):
            for n in range(8):
                ba = fm_mm(wpa_d[n], yaT, "yaT")
                bm = fm_mm(wpm_d[n], hmT, "hmT")
                b0 = fm_mm(wfm_d[FM_G + n], hT, "hT")
                b1 = fm_mm(wfm_d[FM_G + 8 + n], hT, "hT")
                g0, g0k = tf()
                ACT(g0, ps(b0), AF.Sigmoid, [pk(b0), "cf"], [g0k], bias=bfm(FM_G + n))
                g1, g1k = tf()
                ACT(g1, ps(b1), AF.Sigmoid, [pk(b1), "cf"], [g1k], bias=bfm(FM_G + 8 + n))
                TT(g0, ps(ba), g0, ALU.mult, [pk(ba), g0k], [g0k])
                TT(g1, ps(bm), g1, ALU.mult, [pk(bm), g1k], [g1k])
                TT(mgT[:, n, :], g0, g1, ALU.add, [g0k, g1k], ["aqT"])
            for n in range(8):
                b = fm_mm(wo_d[n], mgT, "aqT")
                TT(X["t"][:, n, :], ps(b), X["t"][:, n, :], ALU.add, [pk(b), xk(n)], [xk(n)])
                if hook is not None:
                    if n >= 2:
                        hook(n - 2)
                    if n == 7:
                        hook(6)
                        hook(7)

        def dump(name, tile_ap, key):
            if name in dbg_d:
                P.dma("sp", dbg_d[name], tile_ap, reads=[key], semkey="dbg_" + name)

        tgs = list(range(4 - n_tg_prev, 4)) + list(range(4, 4 + n_tg_own))
        hoisted = {"n0": False}
        n_prev_run = n_tg_prev
        for pos, tg in enumerate(tgs):
            own = tg >= 4
            P.epoch = pos + 1
            t = pos
            X["t"], X["i"] = xbuf[pos % 2], pos % 2
            if pos == 0:
                P.dma("sp", xbuf[0][:], xin[:, :, tg * 512:(tg + 1) * 512].rearrange("k p n -> p k n"),
                      writes=xall(), semkey=("xld", 0))
            if pos + 1 < len(tgs):
                ntg, ni = tgs[pos + 1], (pos + 1) % 2
                P.dma("sp", xbuf[ni][:], xin[:, :, ntg * 512:(ntg + 1) * 512].rearrange("k p n -> p k n"),
                      writes=[("x", ni, kc) for kc in range(8)], semkey=("xld", ni))
            if not hoisted["n0"]:
                norm(0, hT, "hT")
            hoisted["n0"] = False
            na = NormAcc()
            ffn(1, hook=na.add)
            if level >= 2:
                na.finish(1, hT, "hT")
                w_in(t, own)
                if (not own) and pos + 1 < len(tgs):
                    X["t"], X["i"] = xbuf[(pos + 1) % 2], (pos + 1) % 2
                    norm(0, hT, "hT")
                    X["t"], X["i"] = xbuf[pos % 2], pos % 2
                    hoisted["n0"] = True
            if own and level >= 3:
                attention(t)
            if level >= 4:
                mlstm(t, own)
                halo_copy()
            if own and level < 5:
                P.dma("sp", out_d[:, :, (tg - 4) * 512:(tg - 3) * 512].rearrange("k p n -> p k n"), X["t"][:],
                      reads=xall() + ["aqT", "ak2T", "av2", "moT", "qkT", "mv", "yaT", "hmT", "C"], semkey="out")
            if own and level >= 5:
                if tg == 4:
                    dump("ya", yaT[:].rearrange("p k n -> p (k n)"), "yaT")
                    dump("hm", hmT[:].rearrange("p k n -> p (k n)"), "hmT")
                    dump("qk", qkT[:].rearrange("p k n -> p (k n)"), "qkT")
                na = NormAcc()
                proj(hook=na.add)
                if tg == 4:
                    pass
                na.finish(2, hT, "hT")
                na = NormAcc()
                ffn(2, hook=na.add)
                na.finish(3, None, "xT")
                P.dma("sp", out_d[:, :, (tg - 4) * 512:(tg - 3) * 512].rearrange("k p n -> p k n"), X["t"][:],
                      reads=xall(), semkey="out")
        fw = ["out"] + ["dbg_" + n for n in dbg_d]
        P.emit(final_wait_keys=fw)
    return nc


def _tile_fm(W, cols=None):
    Wc = W if cols is None else W[:, cols]
    n = Wc.shape[1] // 128
    return np.ascontiguousarray(Wc.reshape(8, 128, n, 128).transpose(2, 1, 0, 3)).reshape(n, 128, 1024)


def _fm_cols():
    cols = [np.arange(AQ, AQ + 1024)]
    z = np.full(64, ZC)
    for g in range(4):
        c = np.arange(AK + g * 64, AK + g * 64 + 64)
        cols.append(np.concatenate([c, c]))
    for g in range(4):
        cols.append(np.concatenate([z, z]))
    cols.append(np.arange(MO, MO + 1024))
    for base in (MF, MI):
        for c in range(4):
            cols.append(np.concatenate([np.full(64, base + 2 * c), np.full(64, base + 2 * c + 1)]))
    cols.append(np.arange(MQ, MQ + 512))
    cols.append(np.arange(MK, MK + 512))
    cols.append(np.arange(GP, GP + 2048))
    return np.concatenate(cols)


def _tm_cols():
    cols = []
    for g in range(4):
        c = np.arange(AV + g * 64, AV + g * 64 + 64)
        cols.append(np.concatenate([c, c]))
    cols.append(np.arange(MV, MV + 1024))
    return np.concatenate(cols)


def _prep_common(inp):
    f = lambda a: np.ascontiguousarray(np.asarray(a, dtype=np.float32))
    w_in = np.concatenate([f(inp["w_in"])[0], np.zeros((1024, 1), np.float32)], axis=1)
    b_in = np.concatenate([f(inp["b_in"])[0], np.zeros(1, np.float32)])
    fmc, tmc = _fm_cols(), _tm_cols()
    com = {}
    for i, k in ((1, "ffn1"), (2, "ffn2")):
        com["wg%d" % i] = _tile_fm(f(inp[k + "_w_gate"])[0])
        com["wu%d" % i] = _tile_fm(f(inp[k + "_w_up"])[0])
        com["wd%d" % i] = np.ascontiguousarray(f(inp[k + "_w_down"])[0].reshape(NMF, 128, 1024))
    com["wfm"] = _tile_fm(w_in, fmc)
    wt = w_in[:, tmc]
    com["wtm"] = np.ascontiguousarray(wt.reshape(8, 128, 6, 256).transpose(2, 1, 0, 3)).reshape(6, 128, 2048)
    gcols = np.concatenate([np.arange(MF, MF + 8), np.arange(MI, MI + 8)])
    com["wgt"] = np.ascontiguousarray(w_in[:, gcols].reshape(8, 128, 16).transpose(1, 0, 2)).reshape(128, 128)
    com["wpa"] = _tile_fm(f(inp["w_proj_attn"])[0])
    com["wpm"] = _tile_fm(f(inp["w_proj_mlstm"])[0])
    com["wo"] = _tile_fm(f(inp["w_out"])[0])
    cf = np.zeros((128, NCF), np.float32)
    cf[:, C_BFM:C_BFM + NFM] = b_in[fmc].reshape(NFM, 128).T
    gains = [f(inp["ffn1_norm"])[0], f(inp["mix_norm"])[0], f(inp["ffn2_norm"])[0], f(inp["final_norm"])]
    for gi, g in enumerate(gains):
        cf[:, C_GAIN + gi * 8:C_GAIN + gi * 8 + 8] = g.reshape(8, 128).T
    cf[:, C_HN:C_HN + 8] = f(inp["mlstm_head_norm"])[0].T
    conv = f(inp["mlstm_conv"])[0]
    cf[:, C_CONV:C_CONV + 32] = conv.reshape(4, 8, 128).transpose(2, 1, 0).reshape(128, 32)
    sinks = f(inp["attn_sinks"])[0]
    perm = [4 * g + 2 * cc + bi for g in range(4) for bi in range(2) for cc in range(2)]
    cf[:, C_SINK:C_SINK + 16] = sinks[perm][None, :]
    cf[:, C_BTM:C_BTM + 1536] = b_in[tmc][None, :]
    sm = np.ones(512, np.float32)
    sm[::128] = 0.0
    cf[:, C_SCAN:C_SCAN + 512] = sm[None, :]
    cf[:, C_EPS] = 1e-6
    cf[:, C_ONE] = 1.0
    cf[:, C_LN8] = np.float32(np.log(0.125))
    cf[:, C_TINY] = 1e-30
    cf[0:8, C_BG] = b_in[MF:MF + 8]
    cf[0:8, C_BG + 1] = b_in[MI:MI + 8]
    for c in range(4):
        for p in range(128):
            cf[2 * c + p // 64, C_SEL + c * 128 + p] = 1.0
    k = np.arange(128)
    cb = np.zeros((128, 512), np.float32)
    cb[:, 0:128] = (k[:, None] <= k[None, :])
    cb[:, 128:256] = (k[:, None] > k[None, :])
    cb[:, 256:384] = np.eye(128, dtype=np.float32)
    cb[:, 384:512] = 1.0
    com["cb"] = cb
    return com, cf


def _core_inputs(x, com, cf, core):
    b, hf = core // 2, core % 2
    xin = np.zeros((1024, 4096), np.float32)
    if hf == 1:
        xin[:, 0:2048] = x[b, 0:2048].T
    xin[:, 2048:4096] = x[b, hf * 2048:(hf + 1) * 2048].T
    cfc = cf.copy()
    cfc[:, C_FLAG] = float(hf)
    d = dict(com)
    d["xT_in"] = np.ascontiguousarray(xin.reshape(8, 128, 4096))
    d["cf"] = cfc
    return d


def kernel(**inputs):
    x = np.asarray(inputs["x"], dtype=np.float32)
    com, cf = _prep_common(inputs)
    nc = build_program()
    in_maps = [_core_inputs(x, com, cf, c) for c in range(8)]
    res = run_bass_kernel_spmd(nc, in_maps, core_ids=list(range(8)))
    out = np.empty((4, 4096, 1024), np.float32)
    for c in range(8):
        o = np.asarray(res.results[c]["outT"]).reshape(1024, 2048)
        out[c // 2, (c % 2) * 2048:(c % 2 + 1) * 2048, :] = o.T
    return out
```
